# Optimizing a Trainium2 kernel written in Bass

```python
import math
import jax, jax.numpy as jnp
from jax import lax
import numpy as np

D_MODEL = 1024
BATCH = 4
SEQ = 4096
DEPTH = 4
DEC_BATCH = 8
DEC_SEQ = 2048
PAST_LEN = 128

N_META = 16
D_MIX = D_MODEL
A_WIDTH = 3 * D_MIX // 8
A_HEADS = 6
A_BLOCK = A_WIDTH // A_HEADS
CONV_WIDTH = 4
CONV_LEFT = 2
RG_C = 8.0
B_WIDTH = 3 * D_MIX // 8
B_HEADS = 6
B_HEAD_DIM = B_WIDTH // B_HEADS
HGRN_CHUNK = 64
C_WIDTH = D_MIX - A_WIDTH - B_WIDTH
C_GROUP = 16
C_GROUPS = C_WIDTH // C_GROUP
C_STATE = 64
D_FF = 4 * D_MODEL
EPS = 1e-6
IN_WIDTHS = (A_WIDTH, A_WIDTH, B_WIDTH, B_WIDTH, B_WIDTH, B_WIDTH, B_WIDTH, C_WIDTH)
D_IN = 2 * A_WIDTH + 5 * B_WIDTH + C_WIDTH

kernel_name = "hybrid_bidir_rglru_hgrn2_s5_encoder"


def rms_norm(x, gain=None):
    x32 = x.astype(jnp.float32)
    y = x32 * lax.rsqrt(jnp.mean(x32 * x32, axis=-1, keepdims=True) + EPS)
    if gain is not None:
        y = y * gain.astype(jnp.float32)
    return y.astype(x.dtype)


def linear_scan(a, b, reverse):
    def combine(left, right):
        a_l, b_l = left
        a_r, b_r = right
        return a_l * a_r, a_r * b_l + b_r
    _, h = lax.associative_scan(combine, (a, b), axis=1, reverse=reverse)
    return h


def rglru_branch(xa, ga, conv_w, conv_b, wr, br, wi, bi, lam):
    Bn, T, _ = xa.shape
    xp = jnp.pad(xa.astype(jnp.float32), ((0, 0), (CONV_LEFT, CONV_WIDTH - 1 - CONV_LEFT), (0, 0)))
    xc = conv_b.astype(jnp.float32)
    for j in range(CONV_WIDTH):
        xc = xc + xp[:, j:j + T] * conv_w[j].astype(jnp.float32)
    xh = xc.reshape(Bn, T, A_HEADS, A_BLOCK)
    h_sum = jnp.zeros_like(xc)
    for d in range(2):
        r = jax.nn.sigmoid(jnp.einsum('btni,nij->btnj', xh, wr[d].astype(jnp.float32)).reshape(Bn, T, A_WIDTH) + br[d].astype(jnp.float32))
        i = jax.nn.sigmoid(jnp.einsum('btni,nij->btnj', xh, wi[d].astype(jnp.float32)).reshape(Bn, T, A_WIDTH) + bi[d].astype(jnp.float32))
        log_a = -RG_C * r * jax.nn.softplus(-lam[d].astype(jnp.float32))
        a = jnp.exp(log_a)
        b = jnp.sqrt(-jnp.expm1(2.0 * log_a)) * (i * xc)
        h_sum = h_sum + linear_scan(a, b, reverse=(d == 1))
    y = h_sum * jax.nn.gelu(ga.astype(jnp.float32))
    return y.astype(xa.dtype)


def hgrn2_chunk_scan(q, k, v, log_f):
    Bn, T, H, dk = q.shape
    dv = v.shape[-1]
    n_chunks = T // HGRN_CHUNK

    def to_chunks(z):
        return z.reshape(Bn, n_chunks, HGRN_CHUNK, H, z.shape[-1]).transpose(1, 0, 3, 2, 4)

    mask = jnp.tril(jnp.ones((HGRN_CHUNK, HGRN_CHUNK), dtype=bool))[:, :, None]

    def step(S, inp):
        qb, kb, vb, gb = inp
        b = jnp.cumsum(gb, axis=2)
        o_inter = jnp.einsum('bhck,bhkv->bhcv', qb * jnp.exp(b), S)
        rel = b[:, :, :, None, :] - b[:, :, None, :, :]
        decay = jnp.exp(jnp.where(mask, rel, -jnp.inf))
        scores = jnp.einsum('bhtk,bhsk,bhtsk->bhts', qb, kb, decay)
        o_intra = jnp.einsum('bhts,bhsv->bhtv', scores, vb)
        b_last = b[:, :, -1:, :]
        S_new = jnp.exp(b_last[:, :, 0, :])[..., None] * S + jnp.einsum('bhsk,bhsv->bhkv', kb * jnp.exp(b_last - b), vb)
        return S_new, o_inter + o_intra

    S0 = jnp.zeros((Bn, H, dk, dv), jnp.float32)
    _, o = lax.scan(step, S0, (to_chunks(q), to_chunks(k), to_chunks(v), to_chunks(log_f)))
    return o.transpose(1, 0, 3, 2, 4).reshape(Bn, T, H, dv)


def hgrn2_branch(xq, xi, xf_fwd, xf_bwd, xg, lower_bound):
    Bn, T, _ = xq.shape
    pad = (-T) % HGRN_CHUNK

    def heads(z):
        return jnp.pad(z, ((0, 0), (pad, 0), (0, 0))).reshape(Bn, T + pad, B_HEADS, B_HEAD_DIM)

    q = heads(jax.nn.silu(xq.astype(jnp.float32)))
    v = heads(xi.astype(jnp.float32))
    o = jnp.zeros((Bn, T, B_HEADS, B_HEAD_DIM), jnp.float32)
    for d, xf in enumerate((xf_fwd, xf_bwd)):
        lb = lower_bound[d].astype(jnp.float32)
        log_f = jnp.logaddexp(jnp.log(lb), jnp.log1p(-lb) + jax.nn.log_sigmoid(xf.astype(jnp.float32)))
        k = -jnp.expm1(log_f)
        lf, kk = heads(log_f), heads(k)
        if d == 0:
            o_d = hgrn2_chunk_scan(q, kk, v, lf)
        else:
            o_d = jnp.flip(hgrn2_chunk_scan(jnp.flip(q, 1), jnp.flip(kk, 1), jnp.flip(v, 1), jnp.flip(lf, 1)), 1)
        o = o + o_d[:, pad:]
    o = rms_norm(o).reshape(Bn, T, B_WIDTH) * jax.nn.silu(xg.astype(jnp.float32))
    return o.astype(xq.dtype)


def s5_branch(u, a_re, a_im, log_dt, b_re, b_im, c_re, c_im, d_skip, glu_w, glu_b):
    Bn, T, _ = u.shape
    u32 = u.astype(jnp.float32)
    ug = u32.reshape(Bn, T, C_GROUPS, C_GROUP).astype(jnp.complex64)
    y = jnp.zeros((Bn, T, C_GROUPS, C_GROUP), jnp.float32)
    for d in range(2):
        lam = lax.complex(a_re[d].astype(jnp.float32), a_im[d].astype(jnp.float32))
        dt = jnp.exp(log_dt[d].astype(jnp.float32))[:, None]
        a_bar = jnp.exp(lam * dt)
        b_bar = ((a_bar - 1.0) / lam)[..., None] * lax.complex(b_re[d].astype(jnp.float32), b_im[d].astype(jnp.float32))
        bu = jnp.einsum('btgh,gph->btgp', ug, b_bar)
        h = linear_scan(jnp.broadcast_to(a_bar, bu.shape), bu, reverse=(d == 1))
        c = lax.complex(c_re[d].astype(jnp.float32), c_im[d].astype(jnp.float32))
        y = y + jnp.real(jnp.einsum('btgp,ghp->btgh', h, c))
    y = y.reshape(Bn, T, C_WIDTH) + d_skip.astype(jnp.float32) * u32
    z = jax.nn.gelu(y)
    out = z * jax.nn.sigmoid(z @ glu_w.astype(jnp.float32) + glu_b.astype(jnp.float32))
    return out.astype(u.dtype)


def _trunk(x, meta_tokens, norm_mix, w_in, conv_w, conv_b, rg_wr, rg_br, rg_wi, rg_bi, rg_lambda,
           hgrn_lb_logits, s5_a_re, s5_a_im, s5_log_dt, s5_b_re, s5_b_im, s5_c_re, s5_c_im, s5_d,
           s5_glu_w, s5_glu_b, mix_gain, w_out, norm_mlp, w_up, w_down, norm_final):
    Bn = x.shape[0]
    meta = jnp.broadcast_to(meta_tokens.astype(x.dtype)[None], (Bn, N_META, D_MODEL))
    h = jnp.concatenate([meta, x], axis=1)
    lb_c = jnp.cumsum(jax.nn.softmax(hgrn_lb_logits.astype(jnp.float32), axis=1), axis=1)
    lower_bounds = lb_c - lb_c[:, :1]
    split_points = [int(s) for s in np.cumsum(IN_WIDTHS)[:-1]]
    for l in range(DEPTH):
        hn = rms_norm(h, norm_mix[l])
        proj = hn @ w_in[l]
        xa, ga, xq, xi, xff, xfb, xg, xu = jnp.split(proj, split_points, axis=-1)
        ya = rglru_branch(xa, ga, conv_w[l], conv_b[l], rg_wr[l], rg_br[l], rg_wi[l], rg_bi[l], rg_lambda[l])
        yb = hgrn2_branch(xq, xi, xff, xfb, xg, lower_bounds[:, l])
        yc = s5_branch(xu, s5_a_re[l], s5_a_im[l], s5_log_dt[l], s5_b_re[l], s5_b_im[l],
                       s5_c_re[l], s5_c_im[l], s5_d[l], s5_glu_w[l], s5_glu_b[l])
        ym = jnp.concatenate([rms_norm(ya), rms_norm(yb), rms_norm(yc)], axis=-1) * mix_gain[l]
        h = h + (ym @ w_out[l]).astype(h.dtype)
        hn = rms_norm(h, norm_mlp[l])
        h = h + (jnp.square(jax.nn.relu(hn @ w_up[l])) @ w_down[l]).astype(h.dtype)
    return rms_norm(h[:, N_META:], norm_final)


def setup_inputs(seed: int = 0) -> dict:
    key = jax.random.key(seed)
    ks = jax.random.split(key, 29)
    f32 = jnp.float32
    nrm = lambda k, shape, s: jax.random.normal(k, shape, f32) * s
    a0 = jax.random.uniform(ks[11], (DEPTH, 2, A_WIDTH), f32, minval=0.9, maxval=0.999)
    p = a0 ** (1.0 / RG_C)
    rg_lambda = jnp.log(p) - jnp.log1p(-p)
    a_im_base = jnp.broadcast_to(math.pi * jnp.arange(C_STATE, dtype=f32), (DEPTH, 2, C_GROUPS, C_STATE))
    return {
        "x_prompt": nrm(ks[0], (BATCH, SEQ, D_MODEL), 1.0),
        "x_sample": nrm(ks[1], (DEC_BATCH, DEC_SEQ, D_MODEL), 1.0),
        "meta_tokens": nrm(ks[2], (N_META, D_MODEL), 1.0),
        "norm_mix": 1.0 + nrm(ks[3], (DEPTH, D_MODEL), 0.02),
        "w_in": nrm(ks[4], (DEPTH, D_MODEL, D_IN), D_MODEL ** -0.5),
        "conv_w": nrm(ks[5], (DEPTH, CONV_WIDTH, A_WIDTH), 0.5),
        "conv_b": nrm(ks[6], (DEPTH, A_WIDTH), 0.01),
        "rg_wr": nrm(ks[7], (DEPTH, 2, A_HEADS, A_BLOCK, A_BLOCK), A_BLOCK ** -0.5),
        "rg_br": nrm(ks[8], (DEPTH, 2, A_WIDTH), 0.01),
        "rg_wi": nrm(ks[9], (DEPTH, 2, A_HEADS, A_BLOCK, A_BLOCK), A_BLOCK ** -0.5),
        "rg_bi": nrm(ks[10], (DEPTH, 2, A_WIDTH), 0.01),
        "rg_lambda": rg_lambda,
        "hgrn_lb_logits": nrm(ks[12], (2, DEPTH, B_WIDTH), 0.1),
        "s5_a_re": -0.5 + nrm(ks[13], (DEPTH, 2, C_GROUPS, C_STATE), 0.01),
        "s5_a_im": a_im_base + nrm(ks[14], (DEPTH, 2, C_GROUPS, C_STATE), 0.01),
        "s5_log_dt": jax.random.uniform(ks[15], (DEPTH, 2, C_GROUPS), f32, minval=math.log(1e-3), maxval=math.log(1e-1)),
        "s5_b_re": nrm(ks[16], (DEPTH, 2, C_GROUPS, C_STATE, C_GROUP), (2.0 * C_GROUP) ** -0.5),
        "s5_b_im": nrm(ks[17], (DEPTH, 2, C_GROUPS, C_STATE, C_GROUP), (2.0 * C_GROUP) ** -0.5),
        "s5_c_re": nrm(ks[18], (DEPTH, 2, C_GROUPS, C_GROUP, C_STATE), (2.0 * C_STATE) ** -0.5),
        "s5_c_im": nrm(ks[19], (DEPTH, 2, C_GROUPS, C_GROUP, C_STATE), (2.0 * C_STATE) ** -0.5),
        "s5_d": nrm(ks[20], (DEPTH, C_WIDTH), 0.1),
        "s5_glu_w": nrm(ks[21], (DEPTH, C_WIDTH, C_WIDTH), C_WIDTH ** -0.5),
        "s5_glu_b": nrm(ks[22], (DEPTH, C_WIDTH), 0.01),
        "mix_gain": 1.0 + nrm(ks[23], (DEPTH, D_MIX), 0.02),
        "w_out": nrm(ks[24], (DEPTH, D_MIX, D_MODEL), D_MIX ** -0.5),
        "norm_mlp": 1.0 + nrm(ks[25], (DEPTH, D_MODEL), 0.02),
        "w_up": nrm(ks[26], (DEPTH, D_MODEL, D_FF), D_MODEL ** -0.5),
        "w_down": nrm(ks[27], (DEPTH, D_FF, D_MODEL), D_FF ** -0.5),
        "norm_final": 1.0 + nrm(ks[28], (D_MODEL,), 0.02),
    }


def reference(x_prompt, x_sample, meta_tokens, norm_mix, w_in, conv_w, conv_b, rg_wr, rg_br, rg_wi, rg_bi,
              rg_lambda, hgrn_lb_logits, s5_a_re, s5_a_im, s5_log_dt, s5_b_re, s5_b_im, s5_c_re, s5_c_im,
              s5_d, s5_glu_w, s5_glu_b, mix_gain, w_out, norm_mlp, w_up, w_down, norm_final):
    params = (meta_tokens, norm_mix, w_in, conv_w, conv_b, rg_wr, rg_br, rg_wi, rg_bi, rg_lambda,
              hgrn_lb_logits, s5_a_re, s5_a_im, s5_log_dt, s5_b_re, s5_b_im, s5_c_re, s5_c_im, s5_d,
              s5_glu_w, s5_glu_b, mix_gain, w_out, norm_mlp, w_up, w_down, norm_final)
    y_prompt = _trunk(x_prompt, *params)
    y_sample = _trunk(x_sample, *params)
    return (y_prompt, y_sample)
```

```python
import numpy as np
from contextlib import ExitStack
import concourse.bass as bass
import concourse.mybir as mybir
from concourse.bass_utils import run_bass_kernel_spmd

F32 = mybir.dt.float32
BF16 = mybir.dt.bfloat16
ALU = mybir.AluOpType
AF = mybir.ActivationFunctionType

D = 1024
DEPTH = 4
T = 4128
SEG = 2064
NSEG = 2
MT = 344
NMT = T // MT
TT = 516
NTT = T // TT
HF = 258
D_IN = 2944
D_FF = 4096
EPS = 1e-6
NPV = 128

ENABLE_A = True
ENABLE_B = True
ENABLE_C = True
NLAYERS = DEPTH
DEBUG = False
HG_STAGE = 9
S5_STAGE = 99
DBG_OUT = {}


def _I(name, *args, **kw):
    return lambda e: getattr(e, name)(*args, **kw)


class Sched:
    def __init__(self, nc, es):
        self.nc = nc
        self.es = es
        self.eng = {'pe': nc.tensor, 'act': nc.scalar, 'dve': nc.vector, 'pool': nc.gpsimd, 'sp': nc.sync}
        self.q = {e: [] for e in self.eng}
        self.cnt = {}
        self.sems = {}
        self.seen = {e: {} for e in self.eng}
        self.w = {}
        self.r = {}
        self.dma_i = {'sp': 0, 'pool': 0}
        self.NSLOT = 6
        self.alias = {}

    def _x(self, keys):
        out = []
        for k in keys:
            if k in self.alias:
                out.extend(self.alias[k])
            else:
                out.append(k)
        return out

    def sem(self, key):
        if key not in self.sems:
            self.sems[key] = self.es.enter_context(self.nc.semaphore("s_" + "_".join(str(k) for k in (key if isinstance(key, tuple) else (key,)))))
            self.cnt[key] = 0
        return self.sems[key]

    def _deps(self, eng, reads, writes):
        need = {}
        for key in reads:
            for k, v in self.w.get(key, {}).items():
                if need.get(k, 0) < v:
                    need[k] = v
        for key in writes:
            for dct in (self.w.get(key, {}), self.r.get(key, {})):
                for k, v in dct.items():
                    if k == eng:
                        continue
                    if need.get(k, 0) < v:
                        need[k] = v
        waits = []
        for k, v in need.items():
            if k == eng and eng == 'pe':
                continue
            if self.seen[eng].get(k, 0) < v:
                self.seen[eng][k] = v
                waits.append((k, v))
        return waits

    def _commit(self, key, n, reads, writes):
        for k in reads:
            self.r.setdefault(k, {})[key] = n
        for k in writes:
            self.w[k] = {key: n}
            self.r[k] = {}

    def op(self, eng, fn, reads=(), writes=()):
        reads, writes = self._x(reads), self._x(writes)
        waits = self._deps(eng, reads, writes)
        self.sem(eng)
        self.cnt[eng] += 1
        n = self.cnt[eng]
        self.q[eng].append((waits, fn, eng, 1))
        self._commit(eng, n, reads, writes)

    def dma(self, qeng, fn, reads=(), writes=()):
        reads, writes = self._x(reads), self._x(writes)
        i = self.dma_i[qeng]
        self.dma_i[qeng] += 1
        key = (qeng, i % self.NSLOT)
        self.sem(key)
        waits = self._deps(qeng, reads, writes)
        prev = self.cnt[key]
        if prev > 0 and self.seen[qeng].get(key, 0) < prev:
            self.seen[qeng][key] = prev
            waits.append((key, prev))
        self.cnt[key] += 16
        n = self.cnt[key]
        self.q[qeng].append((waits, fn, key, 16))
        self._commit(key, n, reads, writes)

    def transfer(self, srcs, dsts):
        srcs, dsts = self._x(srcs), self._x(dsts)
        acc = {}
        for s_ in srcs:
            for dct in (self.w.get(s_, {}), self.r.get(s_, {})):
                for k, v in dct.items():
                    if acc.get(k, 0) < v:
                        acc[k] = v
        for d_ in dsts:
            cur = dict(self.w.get(d_, {}))
            for k, v in acc.items():
                if cur.get(k, 0) < v:
                    cur[k] = v
            self.w[d_] = cur

    def emit(self):
        nc = self.nc
        fin = []
        for key, v in self.cnt.items():
            if v > 0:
                fin.append((key, v))
        with nc.Block() as block:
            for e, name in (('pe', 'tensor'), ('act', 'scalar'), ('dve', 'vector'), ('pool', 'gpsimd'), ('sp', 'sync')):
                def body(engine, e=e):
                    for waits, fn, key, inc in self.q[e]:
                        for k, v in waits:
                            engine.wait_ge(self.sems[k], v)
                        fn(engine).then_inc(self.sems[key], inc)
                    if e == 'sp':
                        for k, v in fin:
                            engine.wait_ge(self.sems[k], v)
                getattr(block, name)(body)


def build_program():
    nc = bass.Bass("TRN2", target_bir_lowering=False)
    es = ExitStack()
    S = Sched(nc, es)

    def din(name, shape):
        return nc.dram_tensor(name, list(shape), F32, kind="ExternalInput").ap()

    xT = din("xT", (D, T))
    flags = din("flags", (128, 20))
    pvec = din("pvec", (DEPTH, 128, NPV))
    w_in = din("w_in", (DEPTH, D, D_IN))
    w_out = din("w_out", (DEPTH, D, D))
    w_up = din("w_up", (DEPTH, D, D_FF))
    w_down = din("w_down", (DEPTH, D_FF, D))
    rg_w = din("rg_w", (DEPTH, 2, 2, 6, 64, 64))
    cst = din("cst", (128, 6, 128))
    s5p = din("s5p", (DEPTH, 128, 16, 3))
    s5B = din("s5B", (DEPTH, 128, 16, 2, 16))
    s5C = din("s5C", (DEPTH, 128, 16, 2, 16))
    glu_w = din("glu_w", (DEPTH, 256, 256))
    idp = din("idp", (128, 8, 240))
    yT = nc.dram_tensor("yT", [D, T], F32, kind="ExternalOutput").ap()
    dbg = nc.dram_tensor("dbg", [8, 128, T], F32, kind="ExternalOutput").ap() if DEBUG else None

    def dbgrow(j, row, key):
        if DEBUG:
            S.dma('sp', _I("dma_start", out=dbg[j, :, :], in_=row), [key], ["dbg"])
    hD = nc.dram_tensor("hD", [D, T], F32, kind="Internal").ap()
    yD = nc.dram_tensor("yD", [D, T], F32, kind="Internal").ap()
    ymD = nc.dram_tensor("ymD", [D, T], BF16, kind="Internal").ap()
    wbD = nc.dram_tensor("wbD", [DEPTH, 9, 128, 8192], BF16, kind="Internal").ap()

    def sb(name, shape, dt):
        return es.enter_context(nc.sbuf_tensor(name, list(shape), dt))

    hn = sb("hn", (128, 8, T), BF16)
    rows = [sb("row%d" % i, (128, T), F32) for i in range(7)]
    pp = sb("pp", (128, 2, NPV), F32)
    pc = sb("pc", (128, 64), F32)
    fl = sb("fl", (128, 20), F32)
    ones_f = sb("ones_f", (128, 128), F32)
    ones_b = sb("ones_b", (128, 128), BF16)
    winb = sb("winb", (128, 2, 8, 128), BF16)
    gw = sb("gw", (128, 12, 128), BF16)
    cf = sb("cf", (128, 8), F32)
    ident_b = sb("ident_b", (128, 128), BF16)
    mblk = sb("mblk", (128, 2, 128), BF16)
    bones = sb("bones", (128, 128), F32)
    hm = sb("hm", (128, 2), F32)
    vt = sb("vt", (128, 34, 128), BF16)
    dl = sb("dl", (128, 68), F32)
    Zt = sb("Zt", (128, 2, 64), F32)
    pe_ = sb("pe_", (128, 24), F32)
    pcb = sb("pcb", (128, 6, 4), F32)
    s5ps = sb("s5ps", (128, 16, 3), F32)
    idp_b = sb("idp_b", (128, 8, 240), BF16)
    ident_f = sb("ident_f", (128, 128), F32)
    glw = sb("glw", (128, 2, 256), BF16)
    tmask = sb("tmask", (128, 2, 128), BF16)
    nhm = sb("nhm", (128, 2), F32)
    ps = [es.enter_context(nc.psum_tensor("ps%d" % i, [128, 512], F32)) for i in range(8)]

    R = [r[:] for r in rows]
    rk = ["row%d" % i for i in range(7)]
    for k_ in rk:
        S.alias[k_] = [(k_, 0), (k_, 1)]

    def hk(keys, h):
        return [((k, h) if k in rk else k) for k in keys]

    def halves(engfn, name, reads, writes, **kw):
        for h in range(2):
            kw2 = {}
            for a, v in kw.items():
                if hasattr(v, "shape") and len(v.shape) == 2 and v.shape[-1] == T:
                    kw2[a] = v[:, h * SEG:(h + 1) * SEG]
                else:
                    kw2[a] = v
            engfn(_I(name, **kw2), hk(reads, h), hk(writes, h))

    def V(fn, reads, writes):
        S.op('dve', fn, reads, writes)

    def A(fn, reads, writes):
        S.op('act', fn, reads, writes)

    def P(fn, reads, writes):
        S.op('pe', fn, reads, writes)

    def G(fn, reads, writes):
        S.op('pool', fn, reads, writes)

    win_i = [0]
    psm_i = [0]
    PSM_MOD = [8]

    def load_win(l, chunk):
        slot = win_i[0] % 2
        win_i[0] += 1
        src = w_in[l, :, chunk * 128:(chunk + 1) * 128].rearrange("(k p) m -> p k m", p=128)
        S.dma('pool', _I("dma_start", out=winb[:, slot, :, :], in_=src),
              reads=(), writes=[("win", slot)])
        return slot

    def proj(l, chunk, evac):
        slot = load_win(l, chunk)
        for i in range(NMT):
            b = psm_i[0] % PSM_MOD[0]
            psm_i[0] += 1
            pk = ("ps", b)
            for k in range(8):
                P(_I("matmul", out=ps[b][:, 0:MT], lhsT=winb[:, slot, k, :],
                                                              rhs=hn[:, k, i * MT:(i + 1) * MT],
                                                              start=(k == 0), stop=(k == 7)),
                  reads=[("win", slot), ("hn", (i * MT) // TT), ("hn", ((i + 1) * MT - 1) // TT)], writes=[pk])
            evac(i, ps[b][:, 0:MT], i * MT, pk)

    def ones_sum(src_row, src_key, dst_row, dst_key, first, fp32=True, lhs=None, lhs_key=None):
        lhsT = lhs if lhs is not None else (ones_f[:] if fp32 else ones_b[:])
        for i in range(NMT):
            b = psm_i[0] % PSM_MOD[0]
            psm_i[0] += 1
            pk = ("ps", b)
            sk_ = (src_key, i // 6) if src_key in rk else src_key
            dk_ = (dst_key, i // 6) if dst_key in rk else dst_key
            P(_I("matmul", out=ps[b][:, 0:MT], lhsT=lhsT, rhs=src_row[:, i * MT:(i + 1) * MT],
                                           start=True, stop=True),
              reads=[sk_] + ([lhs_key] if lhs_key else []), writes=[pk])
            if first:
                V(_I("tensor_copy", out=dst_row[:, i * MT:(i + 1) * MT], in_=ps[b][:, 0:MT]),
                  reads=[pk], writes=[dk_])
            else:
                V(_I("tensor_tensor", out=dst_row[:, i * MT:(i + 1) * MT], in0=ps[b][:, 0:MT],
                                                      in1=dst_row[:, i * MT:(i + 1) * MT], op=ALU.add),
                  reads=[pk, dk_], writes=[dk_])

    def rstd_inplace(row, key, n):
        A(_I("activation", out=row, in_=row, func=AF.Ln, scale=1.0 / n, bias=epsb[:, 0:1]), reads=[key, "consts"], writes=[key])
        A(_I("activation", out=row, in_=row, func=AF.Exp, scale=-0.5), reads=[key], writes=[key])

    epsb = sb("epsb", (128, 4), F32)

    V(_I("memset", ones_f[:], 1.0), (), ["consts0"])
    V(_I("memset", ones_b[:], 1.0), (), ["consts0"])
    V(_I("memset", epsb[:, 0:1], EPS), (), ["consts0"])
    V(_I("memset", epsb[:, 1:2], 1.0), (), ["consts"])
    V(_I("memset", gw[:], 0.0), (), ["gw"])
    S.dma('sp', _I("dma_start", out=fl[:], in_=flags[:, :]), (), ["fl"])

    def load_params(l):
        S.dma('sp', _I("dma_start", out=pp[:, l % 2, :], in_=pvec[l, :, :]), (), [("pp", l % 2)])

    PV_NMIX, PV_NMLP, PV_GAIN, PV_NFIN = 0, 8, 16, 24
    PV_A = 32

    def ppc(l, col):
        return pp[:, l % 2, col:col + 1]

    def tile_norm(l, src, src_key, gain_col, dst_fn, dst_keys, sq, sq_key, rstd, rstd_key, out_f32=False):
        A(_I("activation", out=sq, in_=src, func=AF.Square), reads=[src_key], writes=[sq_key])
        for hf in range(2):
            b = psm_tt[0] % 8
            psm_tt[0] += 1
            pk = ("ps", b)
            for k in range(8):
                P(_I("matmul", out=ps[b][:, 0:HF], lhsT=ones_b[:], rhs=sq[:, k, hf * HF:(hf + 1) * HF],
                                                      start=(k == 0), stop=(k == 7)),
                  reads=[sq_key, "consts0"], writes=[pk])
            A(_I("activation", out=rstd[:, hf * HF:(hf + 1) * HF], in_=ps[b][:, 0:HF], func=AF.Ln,
                                                 scale=1.0 / D, bias=epsb[:, 0:1]),
              reads=[pk, "consts0"], writes=[rstd_key])
        A(_I("activation", out=rstd, in_=rstd, func=AF.Exp, scale=-0.5), reads=[rstd_key], writes=[rstd_key])
        for k in range(8):
            V(_I("scalar_tensor_tensor", out=dst_fn(k), in0=src[:, k, :], scalar=ppc(l, gain_col + k),
                                                    in1=rstd, op0=ALU.mult, op1=ALU.mult),
              reads=[src_key, rstd_key, ("pp", l % 2)], writes=dst_keys)

    psm_tt = [0]
    for nm in ("h_t", "hn2_t", "sq_t", "rstd_t"):
        S.alias[nm] = [(nm, 0), (nm, 1)]

    def tile_norm_half(l, hf, src, gain_col, dst3, sq, rstd, dst_keys=None):
        hs = slice(hf * HF, (hf + 1) * HF)
        A(_I("activation", out=sq[:, :, hs], in_=src[:, :, hs], func=AF.Square), reads=[("h_t", hf)], writes=[("sq_t", hf)])
        b = psm_tt[0] % 8
        psm_tt[0] += 1
        pk = ("ps", b)
        for k in range(8):
            P(_I("matmul", out=ps[b][:, 0:HF], lhsT=ones_b[:], rhs=sq[:, k, hs], start=(k == 0), stop=(k == 7)),
              reads=[("sq_t", hf), "consts0"], writes=[pk])
        A(_I("activation", out=rstd[:, hs], in_=ps[b][:, 0:HF], func=AF.Ln, scale=1.0 / D, bias=epsb[:, 0:1]),
          reads=[pk, "consts0"], writes=[("rstd_t", hf)])
        A(_I("activation", out=rstd[:, hs], in_=rstd[:, hs], func=AF.Exp, scale=-0.5), reads=[("rstd_t", hf)], writes=[("rstd_t", hf)])
        for k in range(8):
            V(_I("scalar_tensor_tensor", out=dst3[:, k, hs], in0=src[:, k, hs], scalar=ppc(l, gain_col + k),
                 in1=rstd[:, hs], op0=ALU.mult, op1=ALU.mult),
              reads=[("h_t", hf), ("rstd_t", hf), ("pp", l % 2)], writes=(dst_keys if dst_keys is not None else [("hn2_t", hf)]))

    u_t = rows[0][:].bitcast(BF16)
    u_t2 = rows[1][:].bitcast(BF16)

    def u_ap(f, hf):
        base = u_t if f < 16 else u_t2
        ff = f % 16
        return base[:, ff * TT + hf * HF: ff * TT + (hf + 1) * HF]

    h_t = rows[2][:].rearrange("p (k t) -> p k t", k=8)
    r3b = rows[3][:].bitcast(BF16)
    ym_t = r3b[:, 0:8 * TT].rearrange("p (k t) -> p k t", k=8)
    hn2_t = r3b[:, 8 * TT:16 * TT].rearrange("p (k t) -> p k t", k=8)
    wslot = [rows[4][:].bitcast(BF16)[:, 0:8192], rows[5][:].bitcast(BF16)[:, 0:8192]]
    r6b = rows[6][:].bitcast(BF16)
    sq_t = r6b[:, 0:8 * TT].rearrange("p (k t) -> p k t", k=8)
    rstd_t = rows[6][:, 2064:2064 + TT]
    relu_t = [rows[6][:, 2580 + j * HF: 2580 + (j + 1) * HF] for j in range(4)]
    wp_i = [0]

    def load_piece(src3, shape):
        slot = wp_i[0] % 2
        wp_i[0] += 1
        a, b_ = shape
        dst = wslot[slot].rearrange("p (a b) -> p a b", a=a)
        S.dma('pool', _I("dma_start", out=dst, in_=src3), (), [("wp", slot)])
        return slot, dst

    def layer0_norm():
        for i in range(NTT):
            c0 = i * TT
            S.dma('sp', _I("dma_start", out=h_t, in_=xT[:, c0:c0 + TT].rearrange("(k p) t -> p k t", p=128)),
                  (), ["h_t"])
            tile_norm(0, h_t, "h_t", PV_NMIX, lambda k, c0=c0: hn[:, k, c0:c0 + TT], [("hn", i)],
                      sq_t, "sq_t", rstd_t, "rstd_t")
            if i == NTT - 1:
                mask_tail()

    def mask_tail():
        for k in range(8):
            V(_I("tensor_tensor", out=hn[:, k, T - 16:T], in0=hn[:, k, T - 16:T], in1=fl[:, 4:20], op=ALU.mult),
              reads=[("hn", NTT - 1), "fl"], writes=[("hn", NTT - 1)])

    def piece_src(l, piece):
        if piece == 0:
            return w_out[l].rearrange("(k p) m -> p k m", p=128), 8
        if piece <= 4:
            pc_ = piece - 1
            return w_up[l, :, pc_ * 1024:(pc_ + 1) * 1024].rearrange("(k p) m -> p k m", p=128), 8
        pc_ = piece - 5
        return w_down[l, :, pc_ * 256:(pc_ + 1) * 256].rearrange("(f p) m -> p f m", p=128), 32

    conv_i = {}
    vt_flat = vt[:].rearrange("p a b -> p (a b)")

    def conv_rounds(l, n):
        i0 = conv_i.get(l, 0)
        for r in range(i0, min(18, i0 + n)):
            piece, half = divmod(r, 2)
            src3, a = piece_src(l, piece)
            ah = a // 2
            st = vt_flat[:, 0:4096].rearrange("p (a b) -> p a b", a=ah)
            S.dma('pool', _I("dma_start", out=st, in_=src3[:, half * ah:(half + 1) * ah, :]), (), ["vt"])
            S.dma('sp', _I("dma_start", out=wbD[l, piece, :, half * 4096:(half + 1) * 4096], in_=vt_flat[:, 0:4096]), ["vt"], [("wbD", l)])
        conv_i[l] = min(18, i0 + n)

    def load_piece_bf(l, piece, a):
        slot = wp_i[0] % 2
        wp_i[0] += 1
        S.dma('sp', _I("dma_start", out=wslot[slot], in_=wbD[l, piece, :, :]), [("wbD", l)], [("wp", slot)])
        return slot, wslot[slot].rearrange("p (a b) -> p a b", a=a)

    def tt_phase(l, mixer_on):
        hsrc = xT if l == 0 else hD
        last = (l == NLAYERS - 1)
        pre = None
        for i in range(NTT):
            c0 = i * TT
            if pre is None:
                S.dma('sp', _I("dma_start", out=ym_t, in_=ymD[:, c0:c0 + TT].rearrange("(k p) t -> p k t", p=128)),
                      ["ymD"], ["ym_t"])
                slot, wv = load_piece_bf(l, 0, 8)
            else:
                slot, wv = pre
            for hf in range(2):
                S.dma('pool', _I("dma_start", out=h_t[:, :, hf * HF:(hf + 1) * HF],
                                 in_=hsrc[:, c0 + hf * HF:c0 + (hf + 1) * HF].rearrange("(k p) t -> p k t", p=128)),
                      (), [("h_t", hf)])
            if mixer_on:
                for hf in range(2):
                    for m in range(8):
                        b = psm_tt[0] % 8
                        psm_tt[0] += 1
                        pk = ("ps", b)
                        for k in range(8):
                            P(_I("matmul", out=ps[b][:, 0:HF], lhsT=wv[:, k, m * 128:(m + 1) * 128],
                                                                             rhs=ym_t[:, k, hf * HF:(hf + 1) * HF],
                                                                             start=(k == 0), stop=(k == 7)),
                              reads=[("wp", slot), "ym_t"], writes=[pk])
                        V(_I("tensor_tensor", out=h_t[:, m, hf * HF:(hf + 1) * HF], in0=ps[b][:, 0:HF],
                                                                     in1=h_t[:, m, hf * HF:(hf + 1) * HF], op=ALU.add),
                          reads=[pk, ("h_t", hf)], writes=[("h_t", hf)])
            for hf in range(2):
                tile_norm_half(l, hf, h_t, PV_NMLP, hn2_t, sq_t, rstd_t)
            for pc_ in range(4):
                slot, wv = load_piece_bf(l, 1 + pc_, 8)
                for hf in range(2):
                    for f in range(8):
                        b = psm_tt[0] % 8
                        psm_tt[0] += 1
                        pk = ("ps", b)
                        for k in range(8):
                            P(_I("matmul", out=ps[b][:, 0:HF], lhsT=wv[:, k, f * 128:(f + 1) * 128],
                                                                             rhs=hn2_t[:, k, hf * HF:(hf + 1) * HF],
                                                                             start=(k == 0), stop=(k == 7)),
                              reads=[("wp", slot), ("hn2_t", hf)], writes=[pk])
                        rt = relu_t[b % 4]
                        rkk = ("relu", b % 4)
                        A(_I("activation", out=rt, in_=ps[b][:, 0:HF], func=AF.Relu), reads=[pk], writes=[rkk])
                        V(_I("tensor_tensor", out=u_ap(pc_ * 8 + f, hf), in0=ps[b][:, 0:HF], in1=rt, op=ALU.mult),
                          reads=[pk, rkk], writes=["u_t"])
            for pc_ in range(4):
                slot, wv = load_piece_bf(l, 5 + pc_, 32)
                for mm in range(2):
                    m = pc_ * 2 + mm
                    for hf in range(2):
                        b = psm_tt[0] % 8
                        psm_tt[0] += 1
                        pk = ("ps", b)
                        for f in range(32):
                            P(_I("matmul", out=ps[b][:, 0:HF], lhsT=wv[:, f, mm * 128:(mm + 1) * 128],
                                                                               rhs=u_ap(f, hf), start=(f == 0), stop=(f == 31)),
                              reads=[("wp", slot), "u_t"], writes=[pk])
                        V(_I("tensor_tensor", out=h_t[:, m, hf * HF:(hf + 1) * HF], in0=ps[b][:, 0:HF],
                                                                     in1=h_t[:, m, hf * HF:(hf + 1) * HF], op=ALU.add),
                          reads=[pk, "h_t"], writes=["h_t"])
            if i + 1 < NTT:
                c1 = (i + 1) * TT
                S.dma('sp', _I("dma_start", out=ym_t, in_=ymD[:, c1:c1 + TT].rearrange("(k p) t -> p k t", p=128)),
                      ["ymD"], ["ym_t"])
                pre = load_piece_bf(l, 0, 8)
            else:
                pre = None
            if not last:
                for hf in range(2):
                    S.dma('sp', _I("dma_start", out=hD[:, c0 + hf * HF:c0 + (hf + 1) * HF].rearrange("(k p) t -> p k t", p=128),
                                   in_=h_t[:, :, hf * HF:(hf + 1) * HF]),
                          [("h_t", hf)], ["hD"])
                    tile_norm_half(l + 1, hf, h_t, PV_NMIX, hn[:, :, c0:c0 + TT], sq_t, rstd_t, dst_keys=[("hn", i)])
                if i == NTT - 1:
                    mask_tail()
            else:
                o_t = rows[0][:].rearrange("p (k t) -> p k t", k=8)
                tile_norm(l, h_t, "h_t", PV_NFIN, lambda k: o_t[:, k, :], ["u_t"], sq_t, "sq_t", rstd_t, "rstd_t")
                S.dma('sp', _I("dma_start", out=yT[:, c0:c0 + TT].rearrange("(k p) t -> p k t", p=128), in_=o_t),
                      ["u_t"], ["yT"])

    def rglru_params(l):
        for c in range(3):
            base = PV_A + c * 11
            for d in range(2):
                lam = ppc(l, base + 9 + d)
                o1 = pc[:, c * 8 + d:c * 8 + d + 1]
                o2 = pc[:, c * 8 + 2 + d:c * 8 + 3 + d]
                A(_I("activation", out=o1, in_=lam, func=AF.Exp, scale=-1.0), reads=[("pp", l % 2)], writes=["pc"])
                A(_I("activation", out=o1, in_=o1, func=AF.Ln, scale=1.0, bias=epsb[:, 1:2]), reads=["pc", "consts"], writes=["pc"])
                V(_I("tensor_scalar", out=o2, in0=o1, scalar1=-16.0, scalar2=None, op0=ALU.mult), reads=["pc"], writes=["pc"])
                V(_I("tensor_scalar", out=o1, in0=o1, scalar1=-8.0, scalar2=None, op0=ALU.mult), reads=["pc"], writes=["pc"])
            for j, wj in enumerate((0, 1, 3)):
                V(_I("tensor_tensor", out=pc[:, c * 8 + 4 + j:c * 8 + 5 + j], in0=ppc(l, base + wj),
                                                                        in1=fl[:, 0:1], op=ALU.mult),
                  reads=[("pp", l % 2), "fl"], writes=["pc"])
        for c in range(3):
            for kind in range(2):
                for d in range(2):
                    idx = c * 4 + kind * 2 + d
                    for hh in range(2):
                        S.dma('pool', _I("dma_start",
                            out=gw[hh * 64:(hh + 1) * 64, idx, hh * 64:(hh + 1) * 64], in_=rg_w[l, kind, d, 2 * c + hh, :, :]),
                            (), ["gw"])

    def seg_scan(out, a, b, reverse, okey, akey, bkey):
        order = (1, 0) if reverse else (0, 1)
        for n, s in enumerate(order):
            sl = slice(s * SEG, (s + 1) * SEG)
            oo, aa, bb = out[:, sl], a[:, sl], b[:, sl]
            if reverse:
                oo, aa, bb = oo[:, ::-1], aa[:, ::-1], bb[:, ::-1]
            if n == 0:
                V(_I("tensor_tensor_scan", out=oo, data0=aa, data1=bb, initial=0.0, op0=ALU.mult, op1=ALU.add),
                  reads=[(akey, s), (bkey, s)], writes=[(okey, s)])
                col = (SEG) if reverse else (SEG - 1)
                V(_I("tensor_tensor", out=cf[:, 0:1], in0=out[:, col:col + 1], in1=fl[:, 0:1], op=ALU.mult),
                  reads=[(okey, s), "fl"], writes=["cf"])
            else:
                V(_I("tensor_tensor_scan", out=oo, data0=aa, data1=bb, initial=cf[:, 0:1], op0=ALU.mult, op1=ALU.add),
                  reads=[(akey, s), (bkey, s), "cf"], writes=[(okey, s)])

    def evac_copy(dst, dkey, func=AF.Copy):
        def f(i, psap, c0, pk):
            A(_I("activation", out=dst[:, c0:c0 + MT], in_=psap, func=func), reads=[pk], writes=[(dkey, c0 // SEG)])
        return f

    def gelu_rows(x, xk, tmp, tk):
        halves(A, "activation", [xk], [tk], out=tmp, in_=x, func=AF.Square)
        halves(V, "tensor_scalar", [tk], [tk], out=tmp, in0=tmp, scalar1=0.044715, scalar2=1.0, op0=ALU.mult, op1=ALU.add)
        halves(V, "tensor_tensor", [tk, xk], [tk], out=tmp, in0=tmp, in1=x, op=ALU.mult)
        halves(A, "activation", [tk], [tk], out=tmp, in_=tmp, func=AF.Sigmoid, scale=1.5957691216057308)
        halves(V, "tensor_tensor", [tk, xk], [xk], out=x, in0=x, in1=tmp, op=ALU.mult)

    def store_y(row, key, ch):
        S.dma('sp', _I("dma_start", out=yD[ch * 128:(ch + 1) * 128, :], in_=row), [key], ["yD"])

    def finalize_branch(l, chunks, width):
        rstd_inplace(R[6], rk[6], width)
        ymrow = rows[3][:].bitcast(BF16)[:, 0:T]
        for ch in chunks:
            S.dma('sp', _I("dma_start", out=R[2], in_=yD[ch * 128:(ch + 1) * 128, :]), ["yD"], [rk[2]])
            V(_I("scalar_tensor_tensor", out=ymrow, in0=R[2], scalar=ppc(l, PV_GAIN + ch), in1=R[6], op0=ALU.mult, op1=ALU.mult),
              reads=[rk[2], rk[6], ("pp", l % 2)], writes=[rk[3]])
            S.dma('sp', _I("dma_start", out=ymD[ch * 128:(ch + 1) * 128, :], in_=ymrow), [rk[3]], ["ymD"])

    def zero_branch(chunks):
        ymrow = rows[3][:].bitcast(BF16)[:, 0:T]
        V(_I("memset", ymrow, 0.0), (), [rk[3]])
        for ch in chunks:
            S.dma('sp', _I("dma_start", out=ymD[ch * 128:(ch + 1) * 128, :], in_=ymrow), [rk[3]], ["ymD"])

    def rglru(l):
        rglru_params(l)
        xcb = rows[0][:].bitcast(BF16)[:, 0:T]
        for c in range(3):
            base = PV_A + c * 11
            ppk = ("pp", l % 2)
            proj(l, c, evac_copy(R[0], rk[0]))
            conv_rounds(l, 3)
            x3 = R[0].rearrange("p (s t) -> p s t", s=2)
            o3 = R[1].rearrange("p (s t) -> p s t", s=2)
            for h_ in range(2):
                V(_I("tensor_scalar", out=o3[:, h_, :], in0=x3[:, h_, :], scalar1=ppc(l, base + 2), scalar2=ppc(l, base + 4), op0=ALU.mult, op1=ALU.add),
                  reads=[(rk[0], h_), ppk], writes=[(rk[1], h_)])
                for (wj, osl, isl) in ((0, slice(2, SEG), slice(0, SEG - 2)), (1, slice(1, SEG), slice(0, SEG - 1)), (3, slice(0, SEG - 1), slice(1, SEG))):
                    V(_I("scalar_tensor_tensor", out=o3[:, h_, osl], in0=x3[:, h_, isl], scalar=ppc(l, base + wj),
                         in1=o3[:, h_, osl], op0=ALU.mult, op1=ALU.add),
                      reads=[(rk[0], h_), (rk[1], h_), ppk], writes=[(rk[1], h_)])
            for (j, oc, ic, n) in ((0, SEG, SEG - 2, 2), (1, SEG, SEG - 1, 1), (2, SEG - 1, SEG, 1)):
                V(_I("scalar_tensor_tensor", out=R[1][:, oc:oc + n], in0=R[0][:, ic:ic + n], scalar=pc[:, c * 8 + 4 + j:c * 8 + 5 + j],
                                                                            in1=R[1][:, oc:oc + n], op0=ALU.mult, op1=ALU.add),
                  reads=[rk[0], rk[1], "pc"], writes=[rk[1]])
            if l == 0 and c == 0:
                dbgrow(0, R[0], rk[0])
                dbgrow(1, R[1], rk[1])
            halves(A, "activation", [rk[1]], [rk[0]], out=xcb, in_=R[1], func=AF.Copy)
            for d in range(2):
                rr, rrk = (R[2], rk[2]) if d == 0 else (R[5], rk[5])
                cd = pc[:, c * 8 + d:c * 8 + d + 1]
                cd2 = pc[:, c * 8 + 2 + d:c * 8 + 3 + d]
                for kind, dst, dk, bcol in ((0, rr, rrk, base + 5 + d), (1, R[3], rk[3], base + 7 + d)):
                    idx = c * 4 + kind * 2 + d
                    for i in range(NMT):
                        b = psm_i[0] % PSM_MOD[0]
                        psm_i[0] += 1
                        pk = ("ps", b)
                        P(_I("matmul", out=ps[b][:, 0:MT], lhsT=gw[:, idx, :], rhs=xcb[:, i * MT:(i + 1) * MT], start=True, stop=True),
                          reads=["gw", (rk[0], i // 6)], writes=[pk])
                        A(_I("activation", out=dst[:, i * MT:(i + 1) * MT], in_=ps[b][:, 0:MT], func=AF.Sigmoid,
                                                                               bias=ppc(l, bcol), scale=1.0),
                          reads=[pk, ppk], writes=[(dk, i // 6)])
                halves(A, "activation", [rrk, "pc"], [rk[4]], out=R[4], in_=rr, func=AF.Exp, scale=cd)
                halves(A, "activation", [rrk, "pc"], [rrk], out=rr, in_=rr, func=AF.Exp, scale=cd2)
                halves(A, "activation", [rrk, "consts"], [rrk], out=rr, in_=rr, func=AF.Sqrt, scale=-1.0, bias=epsb[:, 1:2])
                halves(V, "tensor_tensor", [rk[3], rrk], [rk[3]], out=R[3], in0=R[3], in1=rr, op=ALU.mult)
                halves(V, "tensor_tensor", [rk[3], rk[1]], [rk[3]], out=R[3], in0=R[3], in1=R[1], op=ALU.mult)
                if d == 1:
                    V(_I("tensor_tensor", out=R[3][:, T - 16:T], in0=R[3][:, T - 16:T], in1=fl[:, 4:20], op=ALU.mult),
                      reads=[(rk[3], 1), "fl"], writes=[(rk[3], 1)])
                if l == 0 and c == 0:
                    dbgrow(2 + 3 * d, R[4], rk[4])
                    dbgrow(3 + 3 * d, R[3], rk[3])
                seg_scan(rr, R[4], R[3], reverse=(d == 1), okey=rrk, akey=rk[4], bkey=rk[3])
                if l == 0 and c == 0:
                    dbgrow(4 + 3 * d, rr, rrk)
            halves(V, "tensor_tensor", [rk[2], rk[5]], [rk[2]], out=R[2], in0=R[2], in1=R[5], op=ALU.add)
            proj(l, 3 + c, evac_copy(R[3], rk[3]))
            conv_rounds(l, 3)
            gelu_rows(R[3], rk[3], R[4], rk[4])
            halves(V, "tensor_tensor", [rk[2], rk[3]], [rk[2]], out=R[2], in0=R[2], in1=R[3], op=ALU.mult)
            store_y(R[2], rk[2], c)
            halves(A, "activation", [rk[2]], [rk[3]], out=R[3], in_=R[2], func=AF.Square)
            ones_sum(R[3], rk[3], R[6], rk[6], first=(c == 0))
        finalize_branch(l, (0, 1, 2), 384)

    S.dma('pool', _I("dma_start", out=ident_b[:], in_=cst[:, 0, :]), (), ["cst"])
    S.dma('pool', _I("dma_start", out=mblk[:, 0, :], in_=cst[:, 1, :]), (), ["cst"])
    S.dma('pool', _I("dma_start", out=mblk[:, 1, :], in_=cst[:, 2, :]), (), ["cst"])
    S.dma('sp', _I("dma_start", out=bones[:], in_=cst[:, 3, :]), (), ["cst"])
    V(_I("memset", hm[:], 0.0), (), ["hm"])
    V(_I("memset", hm[0:64, 0:1], 1.0), (), ["hm"])
    V(_I("memset", hm[64:128, 1:2], 1.0), (), ["hm"])
    PV_LB = 70

    def hgrn_params(l):
        ppk = ("pp", l % 2)
        A(_I("activation", out=pe_[:], in_=pp[:, l % 2, PV_LB:PV_LB + 24], func=AF.Exp), reads=[ppk], writes=["pe_"])
        for d in range(2):
            for c in range(3):
                e = [pe_[:, (d * 4 + lp) * 3 + c:(d * 4 + lp) * 3 + c + 1] for lp in range(4)]
                j = d * 3 + c
                lb, oml, noml, tmp = (pcb[:, j, i:i + 1] for i in range(4))
                V(_I("tensor_tensor", out=tmp, in0=e[0], in1=e[1], op=ALU.add), ["pe_"], ["pcb"])
                V(_I("tensor_tensor", out=tmp, in0=tmp, in1=e[2], op=ALU.add), ["pe_", "pcb"], ["pcb"])
                V(_I("tensor_tensor", out=tmp, in0=tmp, in1=e[3], op=ALU.add), ["pe_", "pcb"], ["pcb"])
                V(_I("reciprocal", out=tmp, in_=tmp), ["pcb"], ["pcb"])
                if l == 0:
                    V(_I("memset", lb, 0.0), (), ["pcb"])
                else:
                    V(_I("tensor_copy", out=lb, in_=e[1]), ["pe_"], ["pcb"])
                    for lp in range(2, l + 1):
                        V(_I("tensor_tensor", out=lb, in0=lb, in1=e[lp], op=ALU.add), ["pe_", "pcb"], ["pcb"])
                    V(_I("tensor_tensor", out=lb, in0=lb, in1=tmp, op=ALU.mult), ["pcb"], ["pcb"])
                V(_I("tensor_scalar", out=oml, in0=lb, scalar1=-1.0, scalar2=1.0, op0=ALU.mult, op1=ALU.add), ["pcb"], ["pcb"])
                V(_I("tensor_scalar", out=noml, in0=lb, scalar1=1.0, scalar2=-1.0, op0=ALU.mult, op1=ALU.add), ["pcb"], ["pcb"])

    BLK = []
    for s_ in range(2):
        for i_ in range(17):
            BLK.append((s_ * SEG + 128 * i_, 128 if i_ < 16 else 16))

    def chunks_of(bl):
        s_, i_ = divmod(bl, 17)
        if i_ < 16:
            return [(s_ * 33 + 2 * i_, 0, 64), (s_ * 33 + 2 * i_ + 1, 1, 64)]
        return [(s_ * 33 + 32, 0, 16)]

    ps4b = ps[4][:].bitcast(BF16)

    HGPS = ["B4", "B5", "B6", "B7"]

    def hgrn(l):
        S.transfer([("ps", b) for b in range(4, 8)], HGPS)
        PSM_MOD[0] = 4
        hgrn_body(l)
        PSM_MOD[0] = 8
        S.transfer(HGPS, [("ps", b) for b in range(4, 8)])

    def hgrn_body(l):
        hgrn_params(l)
        r1b = rows[1][:].bitcast(BF16)
        r2b = rows[2][:].bitcast(BF16)
        r3b_ = rows[3][:].bitcast(BF16)
        r4b = rows[4][:].bitcast(BF16)
        Sbd = [r1b[:, 0:33 * 128].rearrange("p (g m) -> p g m", m=128), r2b[:, 0:33 * 128].rearrange("p (g m) -> p g m", m=128)]
        scm = r2b[:, 4224:4224 + 1024].rearrange("p (h j m) -> p h j m", h=2, j=4)
        ktb = r2b[:, 5248:5248 + 512].rearrange("p (x j m) -> p x j m", x=2, j=2)
        qt = r4b[:, 0:T]
        k0 = r4b[:, T:2 * T]
        k1 = r3b_[:, 0:T]
        kf = r3b_[:, T:2 * T]
        maskrow = r4b[:, 0:T]
        for c in range(3):
            ppk = ("pp", l % 2)
            slot = load_win(l, 9 + c)
            for bl, (c0, n) in enumerate(BLK):
                b = psm_i[0] % PSM_MOD[0]
                psm_i[0] += 1
                col = 0
                pk = ("ps", b)
                for k in range(8):
                    P(_I("matmul", out=ps[b][0:n, col:col + 128], lhsT=hn[:, k, c0:c0 + n], rhs=winb[:, slot, k, :], start=(k == 0), stop=(k == 7)),
                      reads=[("win", slot), ("hn", c0 // TT), ("hn", (c0 + n - 1) // TT)], writes=[pk])
                A(_I("activation", out=vt[0:n, bl, :], in_=ps[b][0:n, col:col + 128], func=AF.Copy), reads=[pk], writes=["vt"])
            proj(l, 6 + c, evac_copy(R[0], rk[0], AF.Silu))
            for d in range(2):
                if HG_STAGE < 2:
                    continue
                j = d * 3 + c
                lb, oml, noml = (pcb[:, j, i:i + 1] for i in range(3))
                proj(l, 12 + 3 * d + c, evac_copy(R[1], rk[1], AF.Sigmoid))
                A(_I("activation", out=R[2], in_=R[1], func=AF.Ln, scale=oml, bias=lb), reads=[rk[1], "pcb"], writes=[rk[2]])
                A(_I("activation", out=R[1], in_=R[1], func=AF.Identity, scale=noml, bias=oml), reads=[rk[1], "pcb"], writes=[rk[1]])
                G(_I("memset", maskrow, 1.0), (), [rk[4]])
                for s_ in range(2):
                    if d == 0:
                        G(_I("memset", maskrow[:, s_ * SEG:s_ * SEG + 2049:64], 0.0), [rk[4]], [rk[4]])
                    else:
                        G(_I("memset", maskrow[:, s_ * SEG + 63:s_ * SEG + 2048:64], 0.0), [rk[4]], [rk[4]])
                        G(_I("memset", maskrow[:, s_ * SEG + 2063:s_ * SEG + 2064], 0.0), [rk[4]], [rk[4]])
                if d == 0:
                    V(_I("tensor_tensor_scan", out=R[3], data0=maskrow, data1=R[2], initial=0.0, op0=ALU.mult, op1=ALU.add),
                      reads=[rk[4], rk[2]], writes=[rk[3]])
                else:
                    V(_I("tensor_tensor_scan", out=R[3][:, ::-1], data0=maskrow[:, ::-1], data1=R[2][:, ::-1], initial=0.0, op0=ALU.mult, op1=ALU.add),
                      reads=[rk[4], rk[2]], writes=[rk[3]])
                V(_I("tensor_scalar", out=R[3], in0=R[3], scalar1=-80.0, scalar2=None, op0=ALU.max), reads=[rk[3]], writes=[rk[3]])
                A(_I("activation", out=R[2], in_=R[3], func=AF.Exp), reads=[rk[3]], writes=[rk[2]])
                for s_ in range(2):
                    if d == 0:
                        V(_I("tensor_copy", out=dl[:, s_ * 33:s_ * 33 + 32], in_=R[2][:, s_ * SEG + 63:s_ * SEG + 2048:64]), reads=[rk[2]], writes=["dl"])
                        V(_I("tensor_copy", out=dl[:, s_ * 33 + 32:s_ * 33 + 33], in_=R[2][:, s_ * SEG + 2063:s_ * SEG + 2064]), reads=[rk[2]], writes=["dl"])
                    else:
                        V(_I("tensor_copy", out=dl[:, s_ * 33:s_ * 33 + 33], in_=R[2][:, s_ * SEG:s_ * SEG + 2049:64]), reads=[rk[2]], writes=["dl"])
                bc = 32 if d == 0 else 33
                V(_I("tensor_tensor", out=dl[:, bc:bc + 1], in0=dl[:, bc:bc + 1], in1=fl[:, 0:1], op=ALU.mult), reads=["dl", "fl"], writes=["dl"])
                V(_I("tensor_tensor", out=qt, in0=R[0], in1=R[2], op=ALU.mult), reads=[rk[0], rk[2]], writes=[rk[4]])
                A(_I("activation", out=R[2], in_=R[3], func=AF.Exp, scale=-1.0), reads=[rk[3]], writes=[rk[2]])
                V(_I("tensor_tensor", out=kf, in0=R[1], in1=R[2], op=ALU.mult), reads=[rk[1], rk[2]], writes=[rk[3]])
                V(_I("tensor_scalar", out=k0, in0=kf, scalar1=hm[:, 0:1], scalar2=None, op0=ALU.mult), reads=[rk[3], "hm"], writes=[rk[4]])
                V(_I("tensor_scalar", out=k1, in0=kf, scalar1=hm[:, 1:2], scalar2=None, op0=ALU.mult), reads=[rk[3], "hm"], writes=[rk[3]])
                if HG_STAGE < 3:
                    continue
                G(_I("memset", Sbd[0], 0.0), (), [rk[1]])
                G(_I("memset", r2b[:, 0:5248 + 512], 0.0), (), [rk[2]])
                S.transfer([rk[1], rk[2]], ["Sbd", "scm", "ktb"])
                order = list(range(34)) if d == 0 else list(range(33, -1, -1))
                prev_g = None
                zi = 0
                ui = 0
                def prefetch_kt(bi):
                    c0, n = BLK[order[bi]]
                    tb = (4, 6)[bi % 2]
                    tk = "B%d" % tb
                    pstb = ps[tb][:].bitcast(BF16)
                    P(_I("transpose", out=pstb[0:n, 0:128], in_=kf[:, c0:c0 + n], identity=ident_b[:]), reads=[rk[3], "cst"], writes=[tk])
                    kj = bi % 2
                    kk = ("ktb", kj)
                    nl = min(n, 64)
                    A(_I("activation", out=ktb[0:nl, 0, kj, :], in_=pstb[0:nl, 0:128], func=AF.Copy), reads=[tk, "ktb"], writes=[kk])
                    if n == 128:
                        A(_I("activation", out=ktb[64:128, 1, kj, :], in_=pstb[64:128, 0:128], func=AF.Copy), reads=[tk, "ktb"], writes=[kk])
                prefetch_kt(0)
                for bi, bl in enumerate(order):
                    c0, n = BLK[bl]
                    kj = bi % 2
                    kk = ("ktb", kj)
                    if bi + 1 < len(order):
                        prefetch_kt(bi + 1)
                    chs = chunks_of(bl)
                    if d == 1:
                        chs = chs[::-1]
                    for (g, par, C) in chs:
                        ub_ = (5, 7, 2, 3)[ui % 4]
                        uj = 0
                        ui += 1
                        uk = ("B%d" % ub_) if ub_ >= 4 else ("ps", ub_)
                        for h in range(2):
                            P(_I("matmul", out=ps[ub_][64 * h:64 * h + 64, uj * 64:(uj + 1) * 64], lhsT=ktb[0:n, par, kj, 64 * h:64 * h + 64],
                                 rhs=vt[0:n, bl, 64 * h:64 * h + 64], start=True, stop=True),
                              reads=[kk, "vt"], writes=[uk])
                        if prev_g is None:
                            V(_I("tensor_copy", out=Zt[:, zi, :], in_=ps[ub_][:, uj * 64:(uj + 1) * 64]), reads=[uk], writes=[("Z", zi)])
                        else:
                            V(_I("scalar_tensor_tensor", out=Zt[:, 1 - zi, :], in0=Zt[:, zi, :], scalar=dl[:, prev_g:prev_g + 1],
                                 in1=ps[ub_][:, uj * 64:(uj + 1) * 64], op0=ALU.mult, op1=ALU.add),
                              reads=[uk, ("Z", zi), "dl"], writes=[("Z", 1 - zi)])
                            zi = 1 - zi
                        gn = g + 1 if d == 0 else g - 1
                        if 0 <= gn < 66:
                            V(_I("tensor_scalar", out=Sbd[gn // 33][0:64, gn % 33, 0:64], in0=Zt[0:64, zi, :],
                                 scalar1=dl[0:64, g:g + 1], scalar2=None, op0=ALU.mult),
                              reads=[("Z", zi), "dl", "Sbd"], writes=[("Sbd", gn)])
                            V(_I("tensor_scalar", out=Sbd[gn // 33][64:128, gn % 33, 64:128], in0=Zt[64:128, zi, :],
                                 scalar1=dl[64:128, g:g + 1], scalar2=None, op0=ALU.mult),
                              reads=[("Z", zi), "dl", "Sbd"], writes=[("Sbd", gn)])
                        prev_g = g
                def scores_block(bl):
                    c0, n = BLK[bl]
                    sj = bl % 4
                    for h in range(2):
                        sb_ = (6 + h) if bl % 2 == 0 else (2 + h)
                        sk = ("B%d" % sb_) if sb_ >= 4 else ("ps", sb_)
                        kh = k0 if h == 0 else k1
                        P(_I("matmul", out=ps[sb_][0:n, 0:n], lhsT=kh[:, c0:c0 + n], rhs=qt[:, c0:c0 + n], start=True, stop=True),
                          reads=[rk[4], rk[3]], writes=[sk])
                        V(_I("tensor_tensor", out=scm[0:n, h, sj, 0:n], in0=ps[sb_][0:n, 0:n], in1=mblk[0:n, d, 0:n], op=ALU.mult),
                          reads=[sk, "cst", "scm"], writes=[("scm", h, sj)])

                def out_block(bl):
                    c0, n = BLK[bl]
                    s_, i_ = divmod(bl, 17)
                    grp = s_ * 5 + i_ // 4
                    ob = grp % 2
                    ocol0 = c0 - (s_ * SEG + (i_ // 4) * 512)
                    ok = ("ps", ob)
                    sj = bl % 4
                    for h in range(2):
                        P(_I("matmul", out=ps[ob][64 * h:64 * h + 64, ocol0:ocol0 + n], lhsT=vt[0:n, bl, 64 * h:64 * h + 64],
                             rhs=scm[0:n, h, sj, 0:n], start=True, stop=False),
                          reads=[("scm", h, sj), "vt"], writes=[ok])
                    chl = chunks_of(bl)
                    for ci_, (g, par, C) in enumerate(chl):
                        cc0 = c0 + 64 * par
                        oc = ocol0 + 64 * par
                        P(_I("matmul", out=ps[ob][:, oc:oc + C], lhsT=Sbd[g // 33][:, g % 33, :], rhs=qt[:, cc0:cc0 + C], start=False, stop=(ci_ == len(chl) - 1)),
                          reads=[("Sbd", g), "Sbd", rk[4]], writes=[ok])
                    if i_ % 4 == 3 or i_ == 16:
                        g0 = s_ * SEG + (i_ // 4) * 512
                        w_ = c0 + n - g0
                        if d == 0:
                            V(_I("tensor_copy", out=R[5][:, g0:g0 + w_], in_=ps[ob][:, 0:w_]), reads=[ok], writes=[rk[5]])
                        else:
                            V(_I("tensor_tensor", out=R[5][:, g0:g0 + w_], in0=ps[ob][:, 0:w_], in1=R[5][:, g0:g0 + w_], op=ALU.add), reads=[ok, rk[5]], writes=[rk[5]])

                if HG_STAGE >= 4:
                    scores_block(0)
                    for bl in range(34):
                        if bl + 1 < 34:
                            scores_block(bl + 1)
                        out_block(bl)
                S.transfer(["Sbd", "scm", "ktb"] + [("Sbd", g) for g in range(66)] + [("scm", h, j_) for h in range(2) for j_ in range(4)] + [("ktb", j_) for j_ in range(2)],
                           [rk[1], rk[2]])
            if HG_STAGE < 5:
                continue
            A(_I("activation", out=R[0], in_=R[5], func=AF.Square), reads=[rk[5]], writes=[rk[0]])
            ones_sum(R[0], rk[0], R[1], rk[1], first=True, lhs=bones[:], lhs_key="cst")
            rstd_inplace(R[1], rk[1], 64)
            V(_I("tensor_tensor", out=R[5], in0=R[5], in1=R[1], op=ALU.mult), reads=[rk[5], rk[1]], writes=[rk[5]])
            proj(l, 18 + c, evac_copy(R[2], rk[2], AF.Silu))
            V(_I("tensor_tensor", out=R[5], in0=R[5], in1=R[2], op=ALU.mult), reads=[rk[5], rk[2]], writes=[rk[5]])
            store_y(R[5], rk[5], 3 + c)
            A(_I("activation", out=R[0], in_=R[5], func=AF.Square), reads=[rk[5]], writes=[rk[0]])
            ones_sum(R[0], rk[0], R[6], rk[6], first=(c == 0))
        if HG_STAGE < 5:
            zero_branch((3, 4, 5))
            return
        finalize_branch(l, (3, 4, 5), 384)

    S.dma('pool', _I("dma_start", out=idp_b[:], in_=idp[:, :, :]), (), ["cst"])
    S.dma('pool', _I("dma_start", out=tmask[:, 0, :], in_=cst[:, 4, :]), (), ["cst"])
    S.dma('pool', _I("dma_start", out=tmask[:, 1, :], in_=cst[:, 5, :]), (), ["cst"])
    S.dma('sp', _I("dma_start", out=ident_f[:], in_=cst[:, 0, :]), (), ["cst"])
    V(_I("tensor_scalar", out=nhm[:], in0=hm[:], scalar1=-1.0, scalar2=None, op0=ALU.mult), ["hm"], ["hm"])
    PV_SD, PV_GB = 94, 96
    hnb = hn[:].rearrange("p k t -> p (k t)")
    hnf = hnb.bitcast(F32)
    Mt = hnb[:, 0:4096].rearrange("p (g m) -> p g m", m=128)
    Cmre = hnb[:, 4096:8192].rearrange("p (g m) -> p g m", m=128)
    Cmim = hnb[:, 8192:12288].rearrange("p (g m) -> p g m", m=128)
    Wtre = hnb[:, 12288:14336].rearrange("p (g m) -> p g m", m=128)
    Wtim = hnb[:, 14336:16384].rearrange("p (g m) -> p g m", m=128)
    u8s = hnb[:, 16384:20512].rearrange("p (g n) -> p g n", n=516)
    y8s = hnb[:, 20512:24640].rearrange("p (g n) -> p g n", n=516)
    Hp = hnb[:, 24640:26704].rearrange("p (g n) -> p g n", n=516)
    HB_ = [[hnf[:, 13352 + 516 * (2 * x + c_):13352 + 516 * (2 * x + c_ + 1)] for c_ in range(2)] for x in range(2)]
    TMP = hnf[:, 8192:8192 + 2048]
    WM = hnf[:, 10240:10240 + 256].rearrange("p (x m) -> p x m", x=2)
    MUL, ADD, SUB = ALU.mult, ALU.add, ALU.subtract

    def tt(out, a, b, op, r, w):
        V(_I("tensor_tensor", out=out, in0=a, in1=b, op=op), r, w)

    def ts(out, a, s1, op0, r, w, s2=None, op1=None):
        if op1 is None:
            V(_I("tensor_scalar", out=out, in0=a, scalar1=s1, scalar2=None, op0=op0), r, w)
        else:
            V(_I("tensor_scalar", out=out, in0=a, scalar1=s1, scalar2=s2, op0=op0, op1=op1), r, w)

    def stt(out, a, sc, b, op0, op1, r, w):
        V(_I("scalar_tensor_tensor", out=out, in0=a, scalar=sc, in1=b, op0=op0, op1=op1), r, w)

    r6 = R[6]
    Ak_re = r6[:, 3600:3760].rearrange("p (t k) -> p t k", k=10)
    Ak_im = r6[:, 3760:3920].rearrange("p (t k) -> p t k", k=10)
    nAk_im = r6[:, 3920:4080].rearrange("p (t k) -> p t k", k=10)
    rr8 = r6[:, 1024 + 16 * 20:1040 + 16 * 20]
    EB = [[hnf[:, 15416 + 516 * c_:15416 + 516 * (c_ + 1)] for c_ in range(2)],
          [hnf[:, 10256 + 516 * c_:10256 + 516 * (c_ + 1)] for c_ in range(2)]]

    def s5_prep(l):
        TK = ["s5tab"]

        def t16(i):
            return r6[:, 1024 + 16 * i:1040 + 16 * i]
        Bt = r6[:, 0:512].rearrange("p (t c h) -> p t c h", t=16, c=2)
        Ct = r6[:, 512:1024].rearrange("p (t c h) -> p t c h", t=16, c=2)
        S.dma('sp', _I("dma_start", out=s5ps[:], in_=s5p[l]), (), ["s5ps"])
        S.dma('sp', _I("dma_start", out=Bt, in_=s5B[l]), (), TK)
        S.dma('sp', _I("dma_start", out=Ct, in_=s5C[l]), (), TK)
        are0, aim0, ldt = s5ps[:, :, 0], s5ps[:, :, 1], s5ps[:, :, 2]
        (dlt, xr, xi, m16, th, t2, acc, sinv, cosv, re, im, ta, tb, i8r, i8i, am1, nr, ni, gre, gim) = (t16(i) for i in range(20))
        RK = TK + ["s5ps"]
        A(_I("activation", out=dlt, in_=ldt, func=AF.Exp), RK, TK)
        tt(xr, are0, dlt, MUL, RK, TK)
        tt(xi, aim0, dlt, MUL, RK, TK)
        A(_I("activation", out=m16, in_=xr, func=AF.Exp, scale=1.0 / 16), TK, TK)
        ts(th, xi, 1.0 / 16, MUL, TK, TK)
        tt(t2, th, th, MUL, TK, TK)
        ts(acc, t2, -1.0 / 39916800, MUL, TK, TK)
        for c_ in (1.0 / 362880, -1.0 / 5040, 1.0 / 120, -1.0 / 6):
            stt(acc, acc, c_, t2, ADD, MUL, TK, TK)
        stt(sinv, acc, 1.0, th, ADD, MUL, TK, TK)
        ts(acc, t2, 1.0 / 479001600, MUL, TK, TK)
        for c_ in (-1.0 / 3628800, 1.0 / 40320, -1.0 / 720, 1.0 / 24, -0.5):
            stt(acc, acc, c_, t2, ADD, MUL, TK, TK)
        ts(cosv, acc, 1.0, ADD, TK, TK)
        tt(re, m16, cosv, MUL, TK, TK)
        tt(im, m16, sinv, MUL, TK, TK)
        for _ in range(4):
            tt(ta, re, re, MUL, TK, TK)
            tt(tb, im, im, MUL, TK, TK)
            stt(im, re, 2.0, im, MUL, MUL, TK, TK)
            tt(re, ta, tb, SUB, TK, TK)
        Pwr = r6[:, 2048:2192].rearrange("p (t m) -> p t m", m=9)
        Pwi = r6[:, 2200:2344].rearrange("p (t m) -> p t m", m=9)
        V(_I("memset", Pwr[:, :, 0], 1.0), (), TK)
        V(_I("memset", Pwi[:, :, 0], 0.0), (), TK)
        V(_I("tensor_copy", out=Pwr[:, :, 1], in_=re), TK, TK)
        V(_I("tensor_copy", out=Pwi[:, :, 1], in_=im), TK, TK)
        for m in range(1, 8):
            tt(ta, Pwr[:, :, m], re, MUL, TK, TK)
            tt(tb, Pwi[:, :, m], im, MUL, TK, TK)
            tt(Pwr[:, :, m + 1], ta, tb, SUB, TK, TK)
            tt(ta, Pwr[:, :, m], im, MUL, TK, TK)
            tt(tb, Pwi[:, :, m], re, MUL, TK, TK)
            tt(Pwi[:, :, m + 1], ta, tb, ADD, TK, TK)
        tt(ta, Pwr[:, :, 8], Pwr[:, :, 8], MUL, TK, TK)
        tt(tb, Pwi[:, :, 8], Pwi[:, :, 8], MUL, TK, TK)
        tt(ta, ta, tb, ADD, TK, TK)
        V(_I("reciprocal", out=ta, in_=ta), TK, TK)
        tt(i8r, Pwr[:, :, 8], ta, MUL, TK, TK)
        stt(i8i, Pwi[:, :, 8], -1.0, ta, MUL, MUL, TK, TK)
        rinv = t16(21)
        A(_I("activation", out=rinv, in_=ta, func=AF.Sqrt), TK, TK)
        V(_I("reciprocal", out=rr8, in_=rinv), TK, TK)
        tt(Ak_re[:, :, 0], Pwr[:, :, 8], rinv, MUL, TK, TK)
        stt(Ak_im[:, :, 0], Pwi[:, :, 8], -1.0, rinv, MUL, MUL, TK, TK)
        for k in range(9):
            tt(ta, Ak_re[:, :, k], Ak_re[:, :, k], MUL, TK, TK)
            tt(tb, Ak_im[:, :, k], Ak_im[:, :, k], MUL, TK, TK)
            tt(Ak_re[:, :, k + 1], ta, tb, SUB, TK, TK)
            stt(Ak_im[:, :, k + 1], Ak_re[:, :, k], 2.0, Ak_im[:, :, k], MUL, MUL, TK, TK)
        ts(nAk_im, Ak_im, -1.0, MUL, TK, TK)
        ts(am1, re, -1.0, ADD, TK, TK)
        tt(ta, am1, are0, MUL, RK, TK)
        tt(tb, im, aim0, MUL, RK, TK)
        tt(nr, ta, tb, ADD, TK, TK)
        tt(ta, im, are0, MUL, RK, TK)
        tt(tb, am1, aim0, MUL, RK, TK)
        tt(ni, ta, tb, SUB, TK, TK)
        tt(ta, are0, are0, MUL, RK, TK)
        tt(tb, aim0, aim0, MUL, RK, TK)
        tt(ta, ta, tb, ADD, TK, TK)
        V(_I("reciprocal", out=ta, in_=ta), TK, TK)
        tt(gre, nr, ta, MUL, TK, TK)
        tt(gim, ni, ta, MUL, TK, TK)
        bre = r6[:, 3000:3256].rearrange("p (t h) -> p t h", h=16)
        bim = r6[:, 3256:3512].rearrange("p (t h) -> p t h", h=16)
        tmp3 = TMP[:, 0:256].rearrange("p (t h) -> p t h", h=16)
        TT_ = ["s5tmp"]
        gre_b = gre.unsqueeze(2).broadcast_to([128, 16, 16])
        gim_b = gim.unsqueeze(2).broadcast_to([128, 16, 16])
        tt(bre, gre_b, Bt[:, :, 0, :], MUL, TK, TK)
        tt(tmp3, gim_b, Bt[:, :, 1, :], MUL, TK, TT_)
        tt(bre, bre, tmp3, SUB, TK + TT_, TK)
        tt(bim, gre_b, Bt[:, :, 1, :], MUL, TK, TK)
        tt(tmp3, gim_b, Bt[:, :, 0, :], MUL, TK, TT_)
        tt(bim, bim, tmp3, ADD, TK + TT_, TK)
        PwWr = r6[:, 2400:2528].rearrange("p (t j) -> p t j", j=8)
        PwWi = r6[:, 2528:2656].rearrange("p (t j) -> p t j", j=8)
        PwCr = r6[:, 2656:2784].rearrange("p (t j) -> p t j", j=8)
        PwCi = r6[:, 2784:2912].rearrange("p (t j) -> p t j", j=8)
        for (dst, src) in ((PwWr, Pwr), (PwWi, Pwi)):
            V(_I("tensor_copy", out=dst[:, 0:8, :], in_=src[:, 0:8, 7::-1]), TK, TK)
            V(_I("tensor_copy", out=dst[:, 8:16, :], in_=src[:, 8:16, 0:8]), TK, TK)
        for (dst, src) in ((PwCr, Pwr), (PwCi, Pwi)):
            V(_I("tensor_copy", out=dst[:, 0:8, :], in_=src[:, 0:8, 1:9]), TK, TK)
            V(_I("tensor_copy", out=dst[:, 8:16, :], in_=src[:, 8:16, 8:0:-1]), TK, TK)
        if S5_STAGE < 3:
            return
        sh4 = [128, 16, 8, 16]
        Wre = R[3][:, 0:2048].rearrange("p (t j h) -> p t j h", t=16, j=8)
        Wim = R[3][:, 2048:4096].rearrange("p (t j h) -> p t j h", t=16, j=8)
        Cre = R[4][:, 0:2048].rearrange("p (t j h) -> p t j h", t=16, j=8)
        Cim = R[4][:, 2048:4096].rearrange("p (t j h) -> p t j h", t=16, j=8)
        Ypr = R[5][:, 0:2048].rearrange("p (t j h) -> p t j h", t=16, j=8)
        Ypi = R[5][:, 2048:4096].rearrange("p (t j h) -> p t j h", t=16, j=8)
        tmp4 = TMP.rearrange("p (t j h) -> p t j h", t=16, j=8)

        def cmul(ore, oim, ar, ai, br_, bi_, okey):
            tt(ore, ar, br_, MUL, TK + [okey], [okey])
            tt(tmp4, ai, bi_, MUL, TK + [okey], TT_)
            tt(ore, ore, tmp4, SUB, [okey] + TT_, [okey])
            tt(oim, ar, bi_, MUL, TK + [okey], [okey])
            tt(tmp4, ai, br_, MUL, TK + [okey], TT_)
            tt(oim, oim, tmp4, ADD, [okey] + TT_, [okey])
        cmul(Wre, Wim, bre.unsqueeze(2).broadcast_to(sh4), bim.unsqueeze(2).broadcast_to(sh4),
             PwWr.unsqueeze(3).broadcast_to(sh4), PwWi.unsqueeze(3).broadcast_to(sh4), rk[3])
        cmul(Cre, Cim, Ct[:, :, 0, :].unsqueeze(2).broadcast_to(sh4), Ct[:, :, 1, :].unsqueeze(2).broadcast_to(sh4),
             PwCr.unsqueeze(3).broadcast_to(sh4), PwCi.unsqueeze(3).broadcast_to(sh4), rk[4])
        i8r4 = i8r.unsqueeze(2).unsqueeze(3).broadcast_to(sh4)
        i8i4 = i8i.unsqueeze(2).unsqueeze(3).broadcast_to(sh4)
        tt(Ypr, Cre, i8r4, MUL, TK + [rk[4]], [rk[5]])
        tt(tmp4, Cim, i8i4, MUL, TK + [rk[4]], TT_)
        tt(Ypr, Ypr, tmp4, SUB, [rk[5]] + TT_, [rk[5]])
        tt(Ypi, Cre, i8i4, MUL, TK + [rk[4]], [rk[5]])
        tt(tmp4, Cim, i8r4, MUL, TK + [rk[4]], TT_)
        tt(Ypi, Ypi, tmp4, ADD, [rk[5]] + TT_, [rk[5]])
        W2 = [R[3][:, 0:2048], R[3][:, 2048:4096]]
        C2 = [R[4][:, 0:2048], R[4][:, 2048:4096]]
        Y2 = [R[5][:, 0:2048], R[5][:, 2048:4096]]
        if S5_STAGE < 4:
            return
        for tile in range(16):
            for comp, dstW in ((0, Wtre), (1, Wtim)):
                b = psm_i[0] % PSM_MOD[0]
                psm_i[0] += 1
                pk = ("ps", b)
                P(_I("transpose", out=ps[b][:, 0:128], in_=W2[comp][:, tile * 128:(tile + 1) * 128], identity=ident_f[:]), [rk[3], "cst"], [pk])
                A(_I("activation", out=dstW[:, tile, :], in_=ps[b][:, 0:128], func=AF.Copy), [pk], ["Wt"])
        for d in range(2):
            for g in range(16):
                tile = d * 8 + g // 2
                gp = g % 2
                idx = d * 16 + g
                sl = slice(tile * 128, (tile + 1) * 128)
                ts(Cmre[:, idx, :], C2[0][:, sl], hm[:, gp:gp + 1], MUL, [rk[4], "hm"], ["Cm"])
                ts(Cmim[:, idx, :], C2[1][:, sl], nhm[:, gp:gp + 1], MUL, [rk[4], "hm"], ["Cm"])
                ts(WM[:, 0, :], W2[0][:, sl], hm[:, gp:gp + 1], MUL, [rk[3], "hm"], ["WM"])
                ts(WM[:, 1, :], W2[1][:, sl], nhm[:, gp:gp + 1], MUL, [rk[3], "hm"], ["WM"])
                b = psm_i[0] % PSM_MOD[0]
                psm_i[0] += 1
                pk = ("ps", b)
                P(_I("matmul", out=ps[b][:, 0:128], lhsT=WM[:, 0, :], rhs=Y2[0][:, sl], start=True, stop=False), ["WM", rk[5]], [pk])
                P(_I("matmul", out=ps[b][:, 0:128], lhsT=WM[:, 1, :], rhs=Y2[1][:, sl], start=False, stop=True), ["WM", rk[5]], [pk])
                tt(Mt[:, idx, :], ps[b][:, 0:128], tmask[:, d, :], MUL, [pk, "cst"], ["Mt"])

    def hs_step(HBt, hkey, seg, d, tile, k, cur):
        c0 = seg * 258
        sh = 1 << k
        nn = 258 - sh
        if d == 0:
            dsl, ssl, rsl = slice(c0 + sh, c0 + 258), slice(c0, c0 + nn), slice(c0, c0 + sh)
        else:
            dsl, ssl, rsl = slice(c0, c0 + nn), slice(c0 + sh, c0 + 258), slice(c0 + nn, c0 + 258)
        o_, n_ = HBt[cur], HBt[1 - cur]
        pr = Ak_re[:, tile, k:k + 1]
        pi = Ak_im[:, tile, k:k + 1]
        npi = nAk_im[:, tile, k:k + 1]
        ok_, nk_ = (hkey, cur), (hkey, 1 - cur)
        stt(n_[0][:, dsl], o_[0][:, ssl], pr, o_[0][:, dsl], MUL, ADD, [ok_, "s5tab"], [nk_])
        stt(n_[0][:, dsl], o_[1][:, ssl], npi, n_[0][:, dsl], MUL, ADD, [ok_, nk_, "s5tab"], [nk_])
        stt(n_[1][:, dsl], o_[1][:, ssl], pr, o_[1][:, dsl], MUL, ADD, [ok_, "s5tab"], [nk_])
        stt(n_[1][:, dsl], o_[0][:, ssl], pi, n_[1][:, dsl], MUL, ADD, [ok_, nk_, "s5tab"], [nk_])
        A(_I("activation", out=n_[0][:, rsl], in_=o_[0][:, rsl], func=AF.Copy), [ok_], [nk_])
        A(_I("activation", out=n_[1][:, rsl], in_=o_[1][:, rsl], func=AF.Copy), [ok_], [nk_])

    def s5_main(l):
        ppk = ("pp", l % 2)
        ubv = [rows[2][:].bitcast(BF16)[:, 0:T], rows[2][:].bitcast(BF16)[:, T:2 * T]]
        for cc in range(2):
            for gq in range(8):
                for s_ in range(2):
                    b = psm_i[0] % PSM_MOD[0]
                    psm_i[0] += 1
                    pk = ("ps", b)
                    for j in range(8):
                        P(_I("matmul", out=ps[b][:, 0:258], lhsT=idp_b[:, gq, (7 - j) * 16:(7 - j) * 16 + 128],
                             rhs=ubv[cc][:, s_ * SEG + j:s_ * SEG + SEG:8], start=(j == 0), stop=(j == 7)),
                          [rk[2], "cst"], [pk])
                    A(_I("activation", out=u8s[:, gq, s_ * 258:(s_ + 1) * 258], in_=ps[b][:, 0:258], func=AF.Copy), [pk], [("u8s", gq)])
            HS = []
            for tl in range(4):
                base = rows[4 + tl // 2][:]
                o0 = (tl % 2) * 2064
                HS.append([[base[:, o0 + 516 * (2 * x + c_):o0 + 516 * (2 * x + c_ + 1)] for c_ in range(2)] for x in range(2)])
            Hp4 = rows[3][:].bitcast(BF16).rearrange("p (t g n) -> p t g n", t=4, g=4)
            S.transfer([rk[4], rk[5], rk[3]], [(("H", tl), x) for tl in range(4) for x in range(2)] + [("Hp", tl, g_) for tl in range(4) for g_ in range(4)])
            S.transfer([("y8s", g_) for g_ in range(8)] + ["s5tmp", "WM"], [("E", 0), ("E", 1)])
            for d in range(2):
                for tl in range(4):
                    tile = d * 8 + cc * 4 + tl
                    for comp, Wt_ in ((0, Wtre), (1, Wtim)):
                        for s_ in range(2):
                            b = psm_i[0] % PSM_MOD[0]
                            psm_i[0] += 1
                            pk = ("ps", b)
                            for gp in range(2):
                                P(_I("matmul", out=ps[b][64 * gp:64 * gp + 64, 0:258], lhsT=Wt_[:, tile, 64 * gp:64 * gp + 64],
                                     rhs=u8s[:, 2 * tl + gp, s_ * 258:(s_ + 1) * 258], start=True, stop=True),
                                  ["Wt", ("u8s", 2 * tl + gp)], [pk])
                            A(_I("activation", out=HS[tl][0][comp][:, s_ * 258:(s_ + 1) * 258], in_=ps[b][:, 0:258], func=AF.Copy), [pk], [(("H", tl), 0)])
                for tl in range(4):
                    tile = d * 8 + cc * 4 + tl
                    Er, Ei = EB[tl % 2]
                    ek = ("E", tl % 2)
                    h0k, h1k = (("H", tl), 0), (("H", tl), 1)
                    D_re, D_im = HS[tl][0]
                    X_re, X_im = HS[tl][1]
                    V(_I("memset", Er[:, 0:1], 1.0), (), [ek])
                    V(_I("memset", Ei[:, 0:1], 0.0), (), [ek])
                    for k in range(10):
                        sh = 1 << k
                        nn = min(sh, 516 - sh)
                        src, dst = slice(0, nn), slice(sh, sh + nn)
                        wr, wi, nwi = Ak_re[:, tile, k:k + 1], Ak_im[:, tile, k:k + 1], nAk_im[:, tile, k:k + 1]
                        ts(Er[:, dst], Er[:, src], wr, MUL, [ek, "s5tab"], [ek])
                        stt(Er[:, dst], Ei[:, src], nwi, Er[:, dst], MUL, ADD, [ek, "s5tab"], [ek])
                        ts(Ei[:, dst], Ei[:, src], wr, MUL, [ek, "s5tab"], [ek])
                        stt(Ei[:, dst], Er[:, src], wi, Ei[:, dst], MUL, ADD, [ek, "s5tab"], [ek])
                    Fr = Er if d == 0 else Er[:, ::-1]
                    Fi = Ei if d == 0 else Ei[:, ::-1]
                    tt(X_re, D_re, Fr, MUL, [h0k, ek], [h1k])
                    tt(X_im, D_re, Fi, MUL, [h0k, ek], [h1k])
                    tt(D_re, D_im, Fi, MUL, [h0k, ek], [h0k])
                    tt(X_re, X_re, D_re, SUB, [h0k, h1k], [h1k])
                    tt(D_re, D_im, Fr, MUL, [h0k, ek], [h0k])
                    tt(X_im, X_im, D_re, ADD, [h0k, h1k], [h1k])
                    rb = rr8[:, tile:tile + 1].broadcast_to([128, 258])
                    first, second = (0, 1) if d == 0 else (1, 0)
                    for comp in range(2):
                        Xc, Gc = HS[tl][1][comp], HS[tl][0][comp]
                        ck = ("cf", tl, comp)
                        cfc = cf[:, 2 * tl + comp:2 * tl + comp + 1]
                        for n_, sg in enumerate((first, second)):
                            sl = slice(sg * 258, (sg + 1) * 258)
                            xo, go = Xc[:, sl], Gc[:, sl]
                            if d == 1:
                                xo, go = xo[:, ::-1], go[:, ::-1]
                            if n_ == 0:
                                V(_I("tensor_tensor_scan", out=go, data0=rb, data1=xo, initial=0.0, op0=MUL, op1=ADD), [h1k, "s5tab"], [h0k])
                                col = 257 if d == 0 else 258
                                tt(cfc, Gc[:, col:col + 1], fl[:, 0:1], MUL, [h0k, "fl"], [ck])
                            else:
                                V(_I("tensor_tensor_scan", out=go, data0=rb, data1=xo, initial=cfc, op0=MUL, op1=ADD), [h1k, "s5tab", ck], [h0k])
                    G_re, G_im = HS[tl][0]
                    T1, T2 = HS[tl][1]
                    osl, isl = (slice(1, 516), slice(0, 515)) if d == 0 else (slice(0, 515), slice(1, 516))
                    for comp in range(2):
                        hp = Hp4[:, tl, d * 2 + comp, :]
                        hkk = ("Hp", tl, d * 2 + comp)
                        if comp == 0:
                            tt(T1, G_re, Fr, MUL, [h0k, ek], [h1k])
                            tt(T2, G_im, Fi, MUL, [h0k, ek], [h1k])
                            tt(hp[:, osl], T1[:, isl], T2[:, isl], ADD, [h1k], [hkk])
                        else:
                            tt(T1, G_im, Fr, MUL, [h0k, ek], [h1k])
                            tt(T2, G_re, Fi, MUL, [h0k, ek], [h1k])
                            tt(hp[:, osl], T1[:, isl], T2[:, isl], SUB, [h1k], [hkk])
                        if d == 0:
                            V(_I("memset", hp[:, 0:1], 0.0), (), [hkk])
                            tt(hp[:, 258:259], hp[:, 258:259], fl[:, 0:1], MUL, [hkk, "fl"], [hkk])
                        else:
                            V(_I("memset", hp[:, 515:516], 0.0), (), [hkk])
                            tt(hp[:, 257:258], hp[:, 257:258], fl[:, 0:1], MUL, [hkk, "fl"], [hkk])
            S.transfer([("E", 0), ("E", 1)], [("y8s", g_) for g_ in range(8)] + ["s5tmp"])
            for tl in range(4):
                for gp in range(2):
                    gq = 2 * tl + gp
                    for s_ in range(2):
                        b = psm_i[0] % PSM_MOD[0]
                        psm_i[0] += 1
                        pk = ("ps", b)
                        hsl = slice(s_ * 258, (s_ + 1) * 258)
                        n_mm = 0
                        for d in range(2):
                            idx = d * 16 + cc * 8 + gq
                            for lhs, rhs, rkeys in ((Mt[:, idx, :], u8s[:, gq, hsl], ["Mt", ("u8s", gq)]),
                                                    (Cmre[:, idx, :], Hp4[:, tl, d * 2, hsl], ["Cm", ("Hp", tl, d * 2)]),
                                                    (Cmim[:, idx, :], Hp4[:, tl, d * 2 + 1, hsl], ["Cm", ("Hp", tl, d * 2 + 1)])):
                                P(_I("matmul", out=ps[b][:, 0:258], lhsT=lhs, rhs=rhs, start=(n_mm == 0), stop=(n_mm == 5)), rkeys, [pk])
                                n_mm += 1
                        A(_I("activation", out=y8s[:, gq, hsl], in_=ps[b][:, 0:258], func=AF.Copy), [pk], [("y8s", gq)])
            S.transfer([(("H", tl), x) for tl in range(4) for x in range(2)] + [("Hp", tl, g_) for tl in range(4) for g_ in range(4)] + [("cf", tl, c_) for tl in range(4) for c_ in range(2)],
                       [rk[4], rk[5], rk[3]])

            for j in range(8):
                for s_ in range(2):
                    b = psm_i[0] % PSM_MOD[0]
                    psm_i[0] += 1
                    pk = ("ps", b)
                    for gq in range(8):
                        P(_I("matmul", out=ps[b][:, 0:258], lhsT=idp_b[:, j, (7 - gq) * 16:(7 - gq) * 16 + 128],
                             rhs=y8s[:, gq, s_ * 258:(s_ + 1) * 258], start=(gq == 0), stop=(gq == 7)),
                          [("y8s", gq), "cst"], [pk])
                    V(_I("tensor_copy", out=R[3][:, s_ * SEG + j * 258:s_ * SEG + (j + 1) * 258], in_=ps[b][:, 0:258]), [pk], [rk[3]])
            for s_ in range(2):
                nat = R[cc][:, s_ * SEG:(s_ + 1) * SEG].rearrange("p (n j) -> p n j", j=8)
                perm = R[3][:, s_ * SEG:(s_ + 1) * SEG].rearrange("p (j n) -> p n j", j=8)
                stt(nat, nat, ppc(l, PV_SD + cc), perm, MUL, ADD, [rk[cc], rk[3], ppk], [rk[cc]])
            gelu_rows(R[cc], rk[cc], R[4], rk[4])
            A(_I("activation", out=ubv[cc], in_=R[cc], func=AF.Copy), [rk[cc]], [rk[2]])
        S.transfer(["s5tab"], [rk[6]])
        S.dma('pool', _I("dma_start", out=glw[:], in_=glu_w[l].rearrange("(k p) m -> p k m", p=128)), (), ["glw"])
        for m in range(2):
            for i in range(NMT):
                b = psm_i[0] % PSM_MOD[0]
                psm_i[0] += 1
                pk = ("ps", b)
                for kc in range(2):
                    P(_I("matmul", out=ps[b][:, 0:MT], lhsT=glw[:, kc, m * 128:(m + 1) * 128], rhs=ubv[kc][:, i * MT:(i + 1) * MT],
                         start=(kc == 0), stop=(kc == 1)), ["glw", rk[2]], [pk])
                A(_I("activation", out=R[5][:, i * MT:(i + 1) * MT], in_=ps[b][:, 0:MT], func=AF.Sigmoid, bias=ppc(l, PV_GB + m), scale=1.0),
                  [pk, ppk], [rk[5]])
            tt(R[m], R[m], R[5], MUL, [rk[m], rk[5]], [rk[m]])
            store_y(R[m], rk[m], 6 + m)
            A(_I("activation", out=R[4], in_=R[m], func=AF.Square), [rk[m]], [rk[4]])
            ones_sum(R[4], rk[4], R[6], rk[6], first=(m == 0))
        finalize_branch(l, (6, 7), 256)

    S5K = ["Mt", "Cm", "Wt", "WM", "s5tmp", ("H", 0), ("H", 1)] + [("u8s", g) for g in range(8)] + [("y8s", g) for g in range(8)] + [("Hp", g) for g in range(4)]
    HNK = [("hn", i) for i in range(NTT)]

    def s5(l):
        for cc in range(2):
            def ev(i, psap, c0, pk, cc=cc):
                A(_I("activation", out=R[cc][:, c0:c0 + MT], in_=psap, func=AF.Copy), [pk], [rk[cc]])
                V(_I("tensor_copy", out=rows[2][:].bitcast(BF16)[:, cc * T + c0:cc * T + c0 + MT], in_=R[cc][:, c0:c0 + MT]), [rk[cc]], [rk[2]])
            proj(l, 21 + cc, ev)
        S.transfer(HNK, S5K)
        S.transfer([rk[6]], ["s5tab"])
        if S5_STAGE >= 2:
            s5_prep(l)
        if S5_STAGE >= 5:
            s5_main_wrap(l)
        else:
            S.transfer(S5K, HNK)
            S.transfer(["s5tab"], [rk[6]])
            zero_branch((6, 7))

    def s5_main_wrap(l):
        s5_main_pre(l)

    def s5_main_pre(l):
        s5_main(l)
        S.transfer(S5K, HNK)

    TTK = ["h_t", "u_t", "ym_t", "hn2_t", "sq_t", "rstd_t", ("wp", 0), ("wp", 1)] + [("relu", j) for j in range(4)]
    load_params(0)
    layer0_norm()
    for l in range(NLAYERS):
        S.transfer(TTK, rk)
        if l + 1 < DEPTH:
            load_params(l + 1)
        if ENABLE_A:
            rglru(l)
        else:
            zero_branch((0, 1, 2))
        conv_rounds(l, 18)
        if ENABLE_B:
            hgrn(l)
        else:
            zero_branch((3, 4, 5))
        if ENABLE_C:
            s5(l)
        else:
            zero_branch((6, 7))
        S.transfer(rk, TTK)
        tt_phase(l, True)
    S.emit()
    es.close()
    return nc


def make_consts():
    c = np.zeros((128, 6, 128), np.float32)
    i = np.arange(128)
    jj = i // 16
    c[:, 4, :] = jj[:, None] <= jj[None, :]
    c[:, 5, :] = jj[:, None] >= jj[None, :]
    c[:, 0, :] = np.eye(128)
    same = (i[:, None] // 64) == (i[None, :] // 64)
    c[:, 1, :] = same & (i[None, :] >= i[:, None])
    c[:, 2, :] = same & (i[None, :] <= i[:, None])
    c[:, 3, :] = same
    return c


def pack_s5(inp):
    def lay(x):
        l_, d_, g_, p_ = x.shape[:4]
        rest = x.shape[4:]
        y = x.reshape(l_, d_, g_ // 2, 2, p_, *rest)
        y = np.moveaxis(y, (3, 4), (1, 2))
        return np.ascontiguousarray(y.reshape(l_, 2 * p_, d_ * (g_ // 2), *rest))
    a_re, a_im = inp["s5_a_re"], inp["s5_a_im"]
    ldt = np.broadcast_to(inp["s5_log_dt"][..., None], a_re.shape)
    s5p = lay(np.stack([a_re, a_im, ldt], axis=-1))
    s5B = lay(np.stack([inp["s5_b_re"], inp["s5_b_im"]], axis=-2))
    ct = np.stack([np.swapaxes(inp["s5_c_re"], -1, -2), np.swapaxes(inp["s5_c_im"], -1, -2)], axis=-2)
    s5C = lay(ct)
    idp = np.zeros((128, 8, 240), np.float32)
    r = np.arange(128)
    idp[r, r // 16, 112 + r % 16] = 1.0
    return {"s5p": s5p.astype(np.float32), "s5B": s5B.astype(np.float32), "s5C": s5C.astype(np.float32), "idp": idp}


def pack_pvec(inp):
    pv = np.zeros((DEPTH, 128, NPV), np.float32)
    for l in range(DEPTH):
        pv[l, :, 0:8] = inp["norm_mix"][l].reshape(8, 128).T
        pv[l, :, 8:16] = inp["norm_mlp"][l].reshape(8, 128).T
        pv[l, :, 16:24] = inp["mix_gain"][l].reshape(8, 128).T
        pv[l, :, 24:32] = inp["norm_final"].reshape(8, 128).T
        for c in range(3):
            sl = slice(c * 128, (c + 1) * 128)
            base = 32 + c * 11
            for j in range(4):
                pv[l, :, base + j] = inp["conv_w"][l, j, sl]
            pv[l, :, base + 4] = inp["conv_b"][l, sl]
            pv[l, :, base + 5] = inp["rg_br"][l, 0, sl]
            pv[l, :, base + 6] = inp["rg_br"][l, 1, sl]
            pv[l, :, base + 7] = inp["rg_bi"][l, 0, sl]
            pv[l, :, base + 8] = inp["rg_bi"][l, 1, sl]
            pv[l, :, base + 9] = inp["rg_lambda"][l, 0, sl]
            pv[l, :, base + 10] = inp["rg_lambda"][l, 1, sl]
        for d in range(2):
            for lp in range(4):
                for c in range(3):
                    pv[l, :, 70 + (d * 4 + lp) * 3 + c] = inp["hgrn_lb_logits"][d, lp, c * 128:(c + 1) * 128]
        for cc in range(2):
            pv[l, :, 94 + cc] = inp["s5_d"][l, cc * 128:(cc + 1) * 128]
            pv[l, :, 96 + cc] = inp["s5_glu_b"][l, cc * 128:(cc + 1) * 128]
    return pv


_NC_CACHE = {}


def kernel(**inputs):
    inp = {k: np.asarray(v) for k, v in inputs.items()}
    xp, xs, meta = inp["x_prompt"], inp["x_sample"], inp["meta_tokens"]
    if "nc" not in _NC_CACHE:
        _NC_CACHE["nc"] = build_program()
    nc = _NC_CACHE["nc"]
    pv = pack_pvec(inp)
    rg_w = np.ascontiguousarray(np.stack([inp["rg_wr"], inp["rg_wi"]], axis=1))
    shared = {
        "pvec": pv, "w_in": inp["w_in"], "w_out": inp["w_out"], "w_up": inp["w_up"], "w_down": inp["w_down"], "rg_w": rg_w, "cst": make_consts(), "glu_w": inp["s5_glu_w"],
    }
    shared.update(pack_s5(inp))
    in_maps = []
    for c in range(8):
        xT = np.zeros((D, T), np.float32)
        fl = np.zeros((128, 20), np.float32)
        if c < 4:
            xT[:, 0:16] = meta.T
            xT[:, 16:16 + 4096] = xp[c].T
            fl[:, 0] = 1.0
            fl[:, 4:20] = 0.0
        else:
            for s in range(2):
                xT[:, s * SEG:s * SEG + 16] = meta.T
                xT[:, s * SEG + 16:(s + 1) * SEG] = xs[2 * (c - 4) + s].T
            fl[:, 0] = 0.0
            fl[:, 4:20] = 1.0
        fl[:, 1] = 1.0 - fl[:, 0]
        m = {"xT": xT, "flags": fl}
        m.update(shared)
        in_maps.append(m)
    res = run_bass_kernel_spmd(nc, in_maps, core_ids=list(range(8)))
    if DEBUG:
        DBG_OUT["dbg"] = [np.asarray(res.results[c]["dbg"]) for c in range(8)]
    yp = np.zeros((4, 4096, D), np.float32)
    ys = np.zeros((8, 2048, D), np.float32)
    for c in range(8):
        yT = np.asarray(res.results[c]["yT"])
        if c < 4:
            yp[c] = yT[:, 16:16 + 4096].T
        else:
            for s in range(2):
                ys[2 * (c - 4) + s] = yT[:, s * SEG + 16:(s + 1) * SEG].T
    return (yp, ys)
```

```python
import numpy as np
from contextlib import ExitStack
import concourse.bass as bass
import concourse.mybir as mybir
from concourse.bass_utils import run_bass_kernel_spmd

F32 = mybir.dt.float32
BF16 = mybir.dt.bfloat16
ALU = mybir.AluOpType
AF = mybir.ActivationFunctionType

D = 1024
DEPTH = 4
T = 4128
SEG = 2064
NSEG = 2
MT = 344
NMT = T // MT
TT = 516
NTT = T // TT
HF = 258
D_IN = 2944
D_FF = 4096
EPS = 1e-6
NPV = 128

ENABLE_A = True
ENABLE_B = True
ENABLE_C = True
NLAYERS = DEPTH
DEBUG = False
HG_STAGE = 9
S5_STAGE = 99
DBG_OUT = {}


def _I(name, *args, **kw):
    return lambda e: getattr(e, name)(*args, **kw)


class Sched:
    def __init__(self, nc, es):
        self.nc = nc
        self.es = es
        self.eng = {'pe': nc.tensor, 'act': nc.scalar, 'dve': nc.vector, 'pool': nc.gpsimd, 'sp': nc.sync}
        self.q = {e: [] for e in self.eng}
        self.cnt = {}
        self.sems = {}
        self.seen = {e: {} for e in self.eng}
        self.w = {}
        self.r = {}
        self.dma_i = {'sp': 0, 'pool': 0}
        self.NSLOT = 6
        self.alias = {}

    def _x(self, keys):
        out = []
        for k in keys:
            if k in self.alias:
                out.extend(self.alias[k])
            else:
                out.append(k)
        return out

    def sem(self, key):
        if key not in self.sems:
            self.sems[key] = self.es.enter_context(self.nc.semaphore("s_" + "_".join(str(k) for k in (key if isinstance(key, tuple) else (key,)))))
            self.cnt[key] = 0
        return self.sems[key]

    def _deps(self, eng, reads, writes):
        need = {}
        for key in reads:
            for k, v in self.w.get(key, {}).items():
                if need.get(k, 0) < v:
                    need[k] = v
        for key in writes:
            for dct in (self.w.get(key, {}), self.r.get(key, {})):
                for k, v in dct.items():
                    if k == eng:
                        continue
                    if need.get(k, 0) < v:
                        need[k] = v
        waits = []
        for k, v in need.items():
            if k == eng and eng == 'pe':
                continue
            if self.seen[eng].get(k, 0) < v:
                self.seen[eng][k] = v
                waits.append((k, v))
        return waits

    def _commit(self, key, n, reads, writes):
        for k in reads:
            self.r.setdefault(k, {})[key] = n
        for k in writes:
            self.w[k] = {key: n}
            self.r[k] = {}

    def op(self, eng, fn, reads=(), writes=()):
        reads, writes = self._x(reads), self._x(writes)
        waits = self._deps(eng, reads, writes)
        self.sem(eng)
        self.cnt[eng] += 1
        n = self.cnt[eng]
        self.q[eng].append((waits, fn, eng, 1))
        self._commit(eng, n, reads, writes)

    def dma(self, qeng, fn, reads=(), writes=()):
        reads, writes = self._x(reads), self._x(writes)
        i = self.dma_i[qeng]
        self.dma_i[qeng] += 1
        key = (qeng, i % self.NSLOT)
        self.sem(key)
        waits = self._deps(qeng, reads, writes)
        prev = self.cnt[key]
        if prev > 0 and self.seen[qeng].get(key, 0) < prev:
            self.seen[qeng][key] = prev
            waits.append((key, prev))
        self.cnt[key] += 16
        n = self.cnt[key]
        self.q[qeng].append((waits, fn, key, 16))
        self._commit(key, n, reads, writes)

    def transfer(self, srcs, dsts):
        srcs, dsts = self._x(srcs), self._x(dsts)
        acc = {}
        for s_ in srcs:
            for dct in (self.w.get(s_, {}), self.r.get(s_, {})):
                for k, v in dct.items():
                    if acc.get(k, 0) < v:
                        acc[k] = v
        for d_ in dsts:
            cur = dict(self.w.get(d_, {}))
            for k, v in acc.items():
                if cur.get(k, 0) < v:
                    cur[k] = v
            self.w[d_] = cur

    def emit(self):
        nc = self.nc
        fin = []
        for key, v in self.cnt.items():
            if v > 0:
                fin.append((key, v))
        with nc.Block() as block:
            for e, name in (('pe', 'tensor'), ('act', 'scalar'), ('dve', 'vector'), ('pool', 'gpsimd'), ('sp', 'sync')):
                def body(engine, e=e):
                    for waits, fn, key, inc in self.q[e]:
                        for k, v in waits:
                            engine.wait_ge(self.sems[k], v)
                        fn(engine).then_inc(self.sems[key], inc)
                    if e == 'sp':
                        for k, v in fin:
                            engine.wait_ge(self.sems[k], v)
                getattr(block, name)(body)


def build_program():
    nc = bass.Bass("TRN2", target_bir_lowering=False)
    es = ExitStack()
    S = Sched(nc, es)

    def din(name, shape):
        return nc.dram_tensor(name, list(shape), F32, kind="ExternalInput").ap()

    xT = din("xT", (D, T))
    flags = din("flags", (128, 20))
    pvec = din("pvec", (DEPTH, 128, NPV))
    w_in = din("w_in", (DEPTH, D, D_IN))
    w_out = din("w_out", (DEPTH, D, D))
    w_up = din("w_up", (DEPTH, D, D_FF))
    w_down = din("w_down", (DEPTH, D_FF, D))
    rg_w = din("rg_w", (DEPTH, 2, 2, 6, 64, 64))
    cst = din("cst", (128, 6, 128))
    s5p = din("s5p", (DEPTH, 128, 16, 3))
    s5B = din("s5B", (DEPTH, 128, 16, 2, 16))
    s5C = din("s5C", (DEPTH, 128, 16, 2, 16))
    glu_w = din("glu_w", (DEPTH, 256, 256))
    idp = din("idp", (128, 8, 240))
    yT = nc.dram_tensor("yT", [D, T], F32, kind="ExternalOutput").ap()
    dbg = nc.dram_tensor("dbg", [8, 128, T], F32, kind="ExternalOutput").ap() if DEBUG else None

    def dbgrow(j, row, key):
        if DEBUG:
            S.dma('sp', _I("dma_start", out=dbg[j, :, :], in_=row), [key], ["dbg"])
    hD = nc.dram_tensor("hD", [D, T], F32, kind="Internal").ap()
    yD = nc.dram_tensor("yD", [D, T], F32, kind="Internal").ap()
    ymD = nc.dram_tensor("ymD", [D, T], BF16, kind="Internal").ap()
    wbD = nc.dram_tensor("wbD", [DEPTH, 9, 128, 8192], BF16, kind="Internal").ap()

    def sb(name, shape, dt):
        return es.enter_context(nc.sbuf_tensor(name, list(shape), dt))

    hn = sb("hn", (128, 8, T), BF16)
    rows = [sb("row%d" % i, (128, T), F32) for i in range(7)]
    pp = sb("pp", (128, 2, NPV), F32)
    pc = sb("pc", (128, 64), F32)
    fl = sb("fl", (128, 20), F32)
    ones_f = sb("ones_f", (128, 128), F32)
    ones_b = sb("ones_b", (128, 128), BF16)
    winb = sb("winb", (128, 2, 8, 128), BF16)
    gw = sb("gw", (128, 12, 128), BF16)
    cf = sb("cf", (128, 8), F32)
    ident_b = sb("ident_b", (128, 128), BF16)
    mblk = sb("mblk", (128, 2, 128), BF16)
    bones = sb("bones", (128, 128), F32)
    hm = sb("hm", (128, 2), F32)
    vt = sb("vt", (128, 34, 128), BF16)
    dl = sb("dl", (128, 68), F32)
    Zt = sb("Zt", (128, 2, 64), F32)
    pe_ = sb("pe_", (128, 24), F32)
    pcb = sb("pcb", (128, 6, 4), F32)
    s5ps = sb("s5ps", (128, 16, 3), F32)
    idp_b = sb("idp_b", (128, 8, 240), BF16)
    ident_f = sb("ident_f", (128, 128), F32)
    glw = sb("glw", (128, 2, 256), BF16)
    tmask = sb("tmask", (128, 2, 128), BF16)
    nhm = sb("nhm", (128, 2), F32)
    ps = [es.enter_context(nc.psum_tensor("ps%d" % i, [128, 512], F32)) for i in range(8)]

    R = [r[:] for r in rows]
    rk = ["row%d" % i for i in range(7)]
    for k_ in rk:
        S.alias[k_] = [(k_, 0), (k_, 1)]

    def hk(keys, h):
        return [((k, h) if k in rk else k) for k in keys]

    def halves(engfn, name, reads, writes, **kw):
        for h in range(2):
            kw2 = {}
            for a, v in kw.items():
                if hasattr(v, "shape") and len(v.shape) == 2 and v.shape[-1] == T:
                    kw2[a] = v[:, h * SEG:(h + 1) * SEG]
                else:
                    kw2[a] = v
            engfn(_I(name, **kw2), hk(reads, h), hk(writes, h))

    def V(fn, reads, writes):
        S.op('dve', fn, reads, writes)

    def A(fn, reads, writes):
        S.op('act', fn, reads, writes)

    def P(fn, reads, writes):
        S.op('pe', fn, reads, writes)

    def G(fn, reads, writes):
        S.op('pool', fn, reads, writes)

    win_i = [0]
    psm_i = [0]
    PSM_MOD = [8]

    def load_win(l, chunk):
        slot = win_i[0] % 2
        win_i[0] += 1
        src = w_in[l, :, chunk * 128:(chunk + 1) * 128].rearrange("(k p) m -> p k m", p=128)
        S.dma('pool', _I("dma_start", out=winb[:, slot, :, :], in_=src),
              reads=(), writes=[("win", slot)])
        return slot

    def proj(l, chunk, evac):
        slot = load_win(l, chunk)
        for i in range(NMT):
            b = psm_i[0] % PSM_MOD[0]
            psm_i[0] += 1
            pk = ("ps", b)
            for k in range(8):
                P(_I("matmul", out=ps[b][:, 0:MT], lhsT=winb[:, slot, k, :],
                                                              rhs=hn[:, k, i * MT:(i + 1) * MT],
                                                              start=(k == 0), stop=(k == 7)),
                  reads=[("win", slot), ("hn", (i * MT) // TT), ("hn", ((i + 1) * MT - 1) // TT)], writes=[pk])
            evac(i, ps[b][:, 0:MT], i * MT, pk)

    def ones_sum(src_row, src_key, dst_row, dst_key, first, fp32=True, lhs=None, lhs_key=None):
        lhsT = lhs if lhs is not None else (ones_f[:] if fp32 else ones_b[:])
        for i in range(NMT):
            b = psm_i[0] % PSM_MOD[0]
            psm_i[0] += 1
            pk = ("ps", b)
            sk_ = (src_key, i // 6) if src_key in rk else src_key
            dk_ = (dst_key, i // 6) if dst_key in rk else dst_key
            P(_I("matmul", out=ps[b][:, 0:MT], lhsT=lhsT, rhs=src_row[:, i * MT:(i + 1) * MT],
                                           start=True, stop=True),
              reads=[sk_] + ([lhs_key] if lhs_key else []), writes=[pk])
            if first:
                V(_I("tensor_copy", out=dst_row[:, i * MT:(i + 1) * MT], in_=ps[b][:, 0:MT]),
                  reads=[pk], writes=[dk_])
            else:
                V(_I("tensor_tensor", out=dst_row[:, i * MT:(i + 1) * MT], in0=ps[b][:, 0:MT],
                                                      in1=dst_row[:, i * MT:(i + 1) * MT], op=ALU.add),
                  reads=[pk, dk_], writes=[dk_])

    def rstd_inplace(row, key, n):
        A(_I("activation", out=row, in_=row, func=AF.Ln, scale=1.0 / n, bias=epsb[:, 0:1]), reads=[key, "consts"], writes=[key])
        A(_I("activation", out=row, in_=row, func=AF.Exp, scale=-0.5), reads=[key], writes=[key])

    epsb = sb("epsb", (128, 4), F32)

    V(_I("memset", ones_f[:], 1.0), (), ["consts0"])
    V(_I("memset", ones_b[:], 1.0), (), ["consts0"])
    V(_I("memset", epsb[:, 0:1], EPS), (), ["consts0"])
    V(_I("memset", epsb[:, 1:2], 1.0), (), ["consts"])
    V(_I("memset", gw[:], 0.0), (), ["gw"])
    S.dma('sp', _I("dma_start", out=fl[:], in_=flags[:, :]), (), ["fl"])

    def load_params(l):
        S.dma('sp', _I("dma_start", out=pp[:, l % 2, :], in_=pvec[l, :, :]), (), [("pp", l % 2)])

    PV_NMIX, PV_NMLP, PV_GAIN, PV_NFIN = 0, 8, 16, 24
    PV_A = 32

    def ppc(l, col):
        return pp[:, l % 2, col:col + 1]

    def tile_norm(l, src, src_key, gain_col, dst_fn, dst_keys, sq, sq_key, rstd, rstd_key, out_f32=False):
        A(_I("activation", out=sq, in_=src, func=AF.Square), reads=[src_key], writes=[sq_key])
        for hf in range(2):
            b = psm_tt[0] % 8
            psm_tt[0] += 1
            pk = ("ps", b)
            for k in range(8):
                P(_I("matmul", out=ps[b][:, 0:HF], lhsT=ones_b[:], rhs=sq[:, k, hf * HF:(hf + 1) * HF],
                                                      start=(k == 0), stop=(k == 7)),
                  reads=[sq_key, "consts0"], writes=[pk])
            A(_I("activation", out=rstd[:, hf * HF:(hf + 1) * HF], in_=ps[b][:, 0:HF], func=AF.Ln,
                                                 scale=1.0 / D, bias=epsb[:, 0:1]),
              reads=[pk, "consts0"], writes=[rstd_key])
        A(_I("activation", out=rstd, in_=rstd, func=AF.Exp, scale=-0.5), reads=[rstd_key], writes=[rstd_key])
        for k in range(8):
            V(_I("scalar_tensor_tensor", out=dst_fn(k), in0=src[:, k, :], scalar=ppc(l, gain_col + k),
                                                    in1=rstd, op0=ALU.mult, op1=ALU.mult),
              reads=[src_key, rstd_key, ("pp", l % 2)], writes=dst_keys)

    psm_tt = [0]
    for nm in ("h_t", "hn2_t", "sq_t", "rstd_t"):
        S.alias[nm] = [(nm, 0), (nm, 1)]

    def tile_norm_half(l, hf, src, gain_col, dst3, sq, rstd, dst_keys=None):
        hs = slice(hf * HF, (hf + 1) * HF)
        A(_I("activation", out=sq[:, :, hs], in_=src[:, :, hs], func=AF.Square), reads=[("h_t", hf)], writes=[("sq_t", hf)])
        b = psm_tt[0] % 8
        psm_tt[0] += 1
        pk = ("ps", b)
        for k in range(8):
            P(_I("matmul", out=ps[b][:, 0:HF], lhsT=ones_b[:], rhs=sq[:, k, hs], start=(k == 0), stop=(k == 7)),
              reads=[("sq_t", hf), "consts0"], writes=[pk])
        A(_I("activation", out=rstd[:, hs], in_=ps[b][:, 0:HF], func=AF.Ln, scale=1.0 / D, bias=epsb[:, 0:1]),
          reads=[pk, "consts0"], writes=[("rstd_t", hf)])
        A(_I("activation", out=rstd[:, hs], in_=rstd[:, hs], func=AF.Exp, scale=-0.5), reads=[("rstd_t", hf)], writes=[("rstd_t", hf)])
        for k in range(8):
            V(_I("scalar_tensor_tensor", out=dst3[:, k, hs], in0=src[:, k, hs], scalar=ppc(l, gain_col + k),
                 in1=rstd[:, hs], op0=ALU.mult, op1=ALU.mult),
              reads=[("h_t", hf), ("rstd_t", hf), ("pp", l % 2)], writes=(dst_keys if dst_keys is not None else [("hn2_t", hf)]))

    u_t = rows[0][:].bitcast(BF16)
    u_t2 = rows[1][:].bitcast(BF16)

    def u_ap(f, hf):
        base = u_t if f < 16 else u_t2
        ff = f % 16
        return base[:, ff * TT + hf * HF: ff * TT + (hf + 1) * HF]

    h_t = rows[2][:].rearrange("p (k t) -> p k t", k=8)
    r3b = rows[3][:].bitcast(BF16)
    ym_t = r3b[:, 0:8 * TT].rearrange("p (k t) -> p k t", k=8)
    hn2_t = r3b[:, 8 * TT:16 * TT].rearrange("p (k t) -> p k t", k=8)
    wslot = [rows[4][:].bitcast(BF16)[:, 0:8192], rows[5][:].bitcast(BF16)[:, 0:8192]]
    r6b = rows[6][:].bitcast(BF16)
    sq_t = r6b[:, 0:8 * TT].rearrange("p (k t) -> p k t", k=8)
    rstd_t = rows[6][:, 2064:2064 + TT]
    relu_t = [rows[6][:, 2580 + j * HF: 2580 + (j + 1) * HF] for j in range(4)]
    wp_i = [0]

    def load_piece(src3, shape):
        slot = wp_i[0] % 2
        wp_i[0] += 1
        a, b_ = shape
        dst = wslot[slot].rearrange("p (a b) -> p a b", a=a)
        S.dma('pool', _I("dma_start", out=dst, in_=src3), (), [("wp", slot)])
        return slot, dst

    def layer0_norm():
        for i in range(NTT):
            c0 = i * TT
            S.dma('sp', _I("dma_start", out=h_t, in_=xT[:, c0:c0 + TT].rearrange("(k p) t -> p k t", p=128)),
                  (), ["h_t"])
            tile_norm(0, h_t, "h_t", PV_NMIX, lambda k, c0=c0: hn[:, k, c0:c0 + TT], [("hn", i)],
                      sq_t, "sq_t", rstd_t, "rstd_t")
            if i == NTT - 1:
                mask_tail()

    def mask_tail():
        for k in range(8):
            V(_I("tensor_tensor", out=hn[:, k, T - 16:T], in0=hn[:, k, T - 16:T], in1=fl[:, 4:20], op=ALU.mult),
              reads=[("hn", NTT - 1), "fl"], writes=[("hn", NTT - 1)])

    def piece_src(l, piece):
        if piece == 0:
            return w_out[l].rearrange("(k p) m -> p k m", p=128), 8
        if piece <= 4:
            pc_ = piece - 1
            return w_up[l, :, pc_ * 1024:(pc_ + 1) * 1024].rearrange("(k p) m -> p k m", p=128), 8
        pc_ = piece - 5
        return w_down[l, :, pc_ * 256:(pc_ + 1) * 256].rearrange("(f p) m -> p f m", p=128), 32

    conv_i = {}
    vt_flat = vt[:].rearrange("p a b -> p (a b)")

    def conv_rounds(l, n):
        i0 = conv_i.get(l, 0)
        for r in range(i0, min(18, i0 + n)):
            piece, half = divmod(r, 2)
            src3, a = piece_src(l, piece)
            ah = a // 2
            st = vt_flat[:, 0:4096].rearrange("p (a b) -> p a b", a=ah)
            S.dma('pool', _I("dma_start", out=st, in_=src3[:, half * ah:(half + 1) * ah, :]), (), ["vt"])
            S.dma('sp', _I("dma_start", out=wbD[l, piece, :, half * 4096:(half + 1) * 4096], in_=vt_flat[:, 0:4096]), ["vt"], [("wbD", l)])
        conv_i[l] = min(18, i0 + n)

    def load_piece_bf(l, piece, a):
        slot = wp_i[0] % 2
        wp_i[0] += 1
        S.dma('sp', _I("dma_start", out=wslot[slot], in_=wbD[l, piece, :, :]), [("wbD", l)], [("wp", slot)])
        return slot, wslot[slot].rearrange("p (a b) -> p a b", a=a)

    def tt_phase(l, mixer_on):
        hsrc = xT if l == 0 else hD
        last = (l == NLAYERS - 1)
        pre = None
        for i in range(NTT):
            c0 = i * TT
            if pre is None:
                S.dma('sp', _I("dma_start", out=ym_t, in_=ymD[:, c0:c0 + TT].rearrange("(k p) t -> p k t", p=128)),
                      ["ymD"], ["ym_t"])
                slot, wv = load_piece_bf(l, 0, 8)
            else:
                slot, wv = pre
            for hf in range(2):
                S.dma('sp', _I("dma_start", out=h_t[:, :, hf * HF:(hf + 1) * HF],
                               in_=hsrc[:, c0 + hf * HF:c0 + (hf + 1) * HF].rearrange("(k p) t -> p k t", p=128)),
                      (), [("h_t", hf)])
            if mixer_on:
                for hf in range(2):
                    for m in range(8):
                        b = psm_tt[0] % 8
                        psm_tt[0] += 1
                        pk = ("ps", b)
                        for k in range(8):
                            P(_I("matmul", out=ps[b][:, 0:HF], lhsT=wv[:, k, m * 128:(m + 1) * 128],
                                                                             rhs=ym_t[:, k, hf * HF:(hf + 1) * HF],
                                                                             start=(k == 0), stop=(k == 7)),
                              reads=[("wp", slot), "ym_t"], writes=[pk])
                        V(_I("tensor_tensor", out=h_t[:, m, hf * HF:(hf + 1) * HF], in0=ps[b][:, 0:HF],
                                                                     in1=h_t[:, m, hf * HF:(hf + 1) * HF], op=ALU.add),
                          reads=[pk, ("h_t", hf)], writes=[("h_t", hf)])
            for hf in range(2):
                tile_norm_half(l, hf, h_t, PV_NMLP, hn2_t, sq_t, rstd_t)
            for pc_ in range(4):
                slot, wv = load_piece_bf(l, 1 + pc_, 8)
                for hf in range(2):
                    for f in range(8):
                        b = psm_tt[0] % 8
                        psm_tt[0] += 1
                        pk = ("ps", b)
                        for k in range(8):
                            P(_I("matmul", out=ps[b][:, 0:HF], lhsT=wv[:, k, f * 128:(f + 1) * 128],
                                                                             rhs=hn2_t[:, k, hf * HF:(hf + 1) * HF],
                                                                             start=(k == 0), stop=(k == 7)),
                              reads=[("wp", slot), ("hn2_t", hf)], writes=[pk])
                        rt = relu_t[b % 4]
                        rkk = ("relu", b % 4)
                        A(_I("activation", out=rt, in_=ps[b][:, 0:HF], func=AF.Relu), reads=[pk], writes=[rkk])
                        V(_I("tensor_tensor", out=u_ap(pc_ * 8 + f, hf), in0=ps[b][:, 0:HF], in1=rt, op=ALU.mult),
                          reads=[pk, rkk], writes=["u_t"])
            for pc_ in range(4):
                slot, wv = load_piece_bf(l, 5 + pc_, 32)
                for mm in range(2):
                    m = pc_ * 2 + mm
                    for hf in range(2):
                        b = psm_tt[0] % 8
                        psm_tt[0] += 1
                        pk = ("ps", b)
                        for f in range(32):
                            P(_I("matmul", out=ps[b][:, 0:HF], lhsT=wv[:, f, mm * 128:(mm + 1) * 128],
                                                                               rhs=u_ap(f, hf), start=(f == 0), stop=(f == 31)),
                              reads=[("wp", slot), "u_t"], writes=[pk])
                        V(_I("tensor_tensor", out=h_t[:, m, hf * HF:(hf + 1) * HF], in0=ps[b][:, 0:HF],
                                                                     in1=h_t[:, m, hf * HF:(hf + 1) * HF], op=ALU.add),
                          reads=[pk, "h_t"], writes=["h_t"])
            if i + 1 < NTT:
                c1 = (i + 1) * TT
                S.dma('sp', _I("dma_start", out=ym_t, in_=ymD[:, c1:c1 + TT].rearrange("(k p) t -> p k t", p=128)),
                      ["ymD"], ["ym_t"])
                pre = load_piece_bf(l, 0, 8)
            else:
                pre = None
            if not last:
                for hf in range(2):
                    S.dma('sp', _I("dma_start", out=hD[:, c0 + hf * HF:c0 + (hf + 1) * HF].rearrange("(k p) t -> p k t", p=128),
                                   in_=h_t[:, :, hf * HF:(hf + 1) * HF]),
                          [("h_t", hf)], ["hD"])
                    tile_norm_half(l + 1, hf, h_t, PV_NMIX, hn[:, :, c0:c0 + TT], sq_t, rstd_t, dst_keys=[("hn", i)])
                if i == NTT - 1:
                    mask_tail()
            else:
                o_t = rows[0][:].rearrange("p (k t) -> p k t", k=8)
                tile_norm(l, h_t, "h_t", PV_NFIN, lambda k: o_t[:, k, :], ["u_t"], sq_t, "sq_t", rstd_t, "rstd_t")
                S.dma('sp', _I("dma_start", out=yT[:, c0:c0 + TT].rearrange("(k p) t -> p k t", p=128), in_=o_t),
                      ["u_t"], ["yT"])

    def rglru_params(l):
        for c in range(3):
            base = PV_A + c * 11
            for d in range(2):
                lam = ppc(l, base + 9 + d)
                o1 = pc[:, c * 8 + d:c * 8 + d + 1]
                o2 = pc[:, c * 8 + 2 + d:c * 8 + 3 + d]
                A(_I("activation", out=o1, in_=lam, func=AF.Exp, scale=-1.0), reads=[("pp", l % 2)], writes=["pc"])
                A(_I("activation", out=o1, in_=o1, func=AF.Ln, scale=1.0, bias=epsb[:, 1:2]), reads=["pc", "consts"], writes=["pc"])
                V(_I("tensor_scalar", out=o2, in0=o1, scalar1=-16.0, scalar2=None, op0=ALU.mult), reads=["pc"], writes=["pc"])
                V(_I("tensor_scalar", out=o1, in0=o1, scalar1=-8.0, scalar2=None, op0=ALU.mult), reads=["pc"], writes=["pc"])
            for j, wj in enumerate((0, 1, 3)):
                V(_I("tensor_tensor", out=pc[:, c * 8 + 4 + j:c * 8 + 5 + j], in0=ppc(l, base + wj),
                                                                        in1=fl[:, 0:1], op=ALU.mult),
                  reads=[("pp", l % 2), "fl"], writes=["pc"])
        for c in range(3):
            for kind in range(2):
                for d in range(2):
                    idx = c * 4 + kind * 2 + d
                    for hh in range(2):
                        S.dma('pool', _I("dma_start",
                            out=gw[hh * 64:(hh + 1) * 64, idx, hh * 64:(hh + 1) * 64], in_=rg_w[l, kind, d, 2 * c + hh, :, :]),
                            (), ["gw"])

    def seg_scan(out, a, b, reverse, okey, akey, bkey):
        order = (1, 0) if reverse else (0, 1)
        for n, s in enumerate(order):
            sl = slice(s * SEG, (s + 1) * SEG)
            oo, aa, bb = out[:, sl], a[:, sl], b[:, sl]
            if reverse:
                oo, aa, bb = oo[:, ::-1], aa[:, ::-1], bb[:, ::-1]
            if n == 0:
                V(_I("tensor_tensor_scan", out=oo, data0=aa, data1=bb, initial=0.0, op0=ALU.mult, op1=ALU.add),
                  reads=[(akey, s), (bkey, s)], writes=[(okey, s)])
                col = (SEG) if reverse else (SEG - 1)
                V(_I("tensor_tensor", out=cf[:, 0:1], in0=out[:, col:col + 1], in1=fl[:, 0:1], op=ALU.mult),
                  reads=[(okey, s), "fl"], writes=["cf"])
            else:
                V(_I("tensor_tensor_scan", out=oo, data0=aa, data1=bb, initial=cf[:, 0:1], op0=ALU.mult, op1=ALU.add),
                  reads=[(akey, s), (bkey, s), "cf"], writes=[(okey, s)])

    def evac_copy(dst, dkey, func=AF.Copy):
        def f(i, psap, c0, pk):
            A(_I("activation", out=dst[:, c0:c0 + MT], in_=psap, func=func), reads=[pk], writes=[(dkey, c0 // SEG)])
        return f

    def gelu_rows(x, xk, tmp, tk):
        halves(A, "activation", [xk], [tk], out=tmp, in_=x, func=AF.Square)
        halves(V, "tensor_scalar", [tk], [tk], out=tmp, in0=tmp, scalar1=0.044715, scalar2=1.0, op0=ALU.mult, op1=ALU.add)
        halves(V, "tensor_tensor", [tk, xk], [tk], out=tmp, in0=tmp, in1=x, op=ALU.mult)
        halves(A, "activation", [tk], [tk], out=tmp, in_=tmp, func=AF.Sigmoid, scale=1.5957691216057308)
        halves(V, "tensor_tensor", [tk, xk], [xk], out=x, in0=x, in1=tmp, op=ALU.mult)

    def store_y(row, key, ch):
        S.dma('sp', _I("dma_start", out=yD[ch * 128:(ch + 1) * 128, :], in_=row), [key], ["yD"])

    def finalize_branch(l, chunks, width):
        rstd_inplace(R[6], rk[6], width)
        for j, ch in enumerate(chunks):
            S.dma('sp', _I("dma_start", out=R[j], in_=yD[ch * 128:(ch + 1) * 128, :]), ["yD"], [rk[j]])
        for j, ch in enumerate(chunks):
            ymrow = rows[3 + j][:].bitcast(BF16)[:, 0:T]
            V(_I("scalar_tensor_tensor", out=ymrow, in0=R[j], scalar=ppc(l, PV_GAIN + ch), in1=R[6], op0=ALU.mult, op1=ALU.mult),
              reads=[rk[j], rk[6], ("pp", l % 2)], writes=[rk[3 + j]])
            S.dma('sp', _I("dma_start", out=ymD[ch * 128:(ch + 1) * 128, :], in_=ymrow), [rk[3 + j]], ["ymD"])

    def zero_branch(chunks):
        ymrow = rows[3][:].bitcast(BF16)[:, 0:T]
        V(_I("memset", ymrow, 0.0), (), [rk[3]])
        for ch in chunks:
            S.dma('sp', _I("dma_start", out=ymD[ch * 128:(ch + 1) * 128, :], in_=ymrow), [rk[3]], ["ymD"])

    def rglru(l):
        rglru_params(l)
        xcb = rows[0][:].bitcast(BF16)[:, 0:T]
        for c in range(3):
            base = PV_A + c * 11
            ppk = ("pp", l % 2)
            proj(l, c, evac_copy(R[0], rk[0]))
            conv_rounds(l, 3)
            x3 = R[0].rearrange("p (s t) -> p s t", s=2)
            o3 = R[1].rearrange("p (s t) -> p s t", s=2)
            for h_ in range(2):
                V(_I("tensor_scalar", out=o3[:, h_, :], in0=x3[:, h_, :], scalar1=ppc(l, base + 2), scalar2=ppc(l, base + 4), op0=ALU.mult, op1=ALU.add),
                  reads=[(rk[0], h_), ppk], writes=[(rk[1], h_)])
                for (wj, osl, isl) in ((0, slice(2, SEG), slice(0, SEG - 2)), (1, slice(1, SEG), slice(0, SEG - 1)), (3, slice(0, SEG - 1), slice(1, SEG))):
                    V(_I("scalar_tensor_tensor", out=o3[:, h_, osl], in0=x3[:, h_, isl], scalar=ppc(l, base + wj),
                         in1=o3[:, h_, osl], op0=ALU.mult, op1=ALU.add),
                      reads=[(rk[0], h_), (rk[1], h_), ppk], writes=[(rk[1], h_)])
            for (j, oc, ic, n) in ((0, SEG, SEG - 2, 2), (1, SEG, SEG - 1, 1), (2, SEG - 1, SEG, 1)):
                V(_I("scalar_tensor_tensor", out=R[1][:, oc:oc + n], in0=R[0][:, ic:ic + n], scalar=pc[:, c * 8 + 4 + j:c * 8 + 5 + j],
                                                                            in1=R[1][:, oc:oc + n], op0=ALU.mult, op1=ALU.add),
                  reads=[rk[0], rk[1], "pc"], writes=[rk[1]])
            if l == 0 and c == 0:
                dbgrow(0, R[0], rk[0])
                dbgrow(1, R[1], rk[1])
            halves(A, "activation", [rk[1]], [rk[0]], out=xcb, in_=R[1], func=AF.Copy)
            for d in range(2):
                rr, rrk = (R[2], rk[2]) if d == 0 else (R[5], rk[5])
                cd = pc[:, c * 8 + d:c * 8 + d + 1]
                cd2 = pc[:, c * 8 + 2 + d:c * 8 + 3 + d]
                for kind, dst, dk, bcol in ((0, rr, rrk, base + 5 + d), (1, R[3], rk[3], base + 7 + d)):
                    idx = c * 4 + kind * 2 + d
                    for i in range(NMT):
                        b = psm_i[0] % PSM_MOD[0]
                        psm_i[0] += 1
                        pk = ("ps", b)
                        P(_I("matmul", out=ps[b][:, 0:MT], lhsT=gw[:, idx, :], rhs=xcb[:, i * MT:(i + 1) * MT], start=True, stop=True),
                          reads=["gw", (rk[0], i // 6)], writes=[pk])
                        A(_I("activation", out=dst[:, i * MT:(i + 1) * MT], in_=ps[b][:, 0:MT], func=AF.Sigmoid,
                                                                               bias=ppc(l, bcol), scale=1.0),
                          reads=[pk, ppk], writes=[(dk, i // 6)])
                halves(A, "activation", [rrk, "pc"], [rk[4]], out=R[4], in_=rr, func=AF.Exp, scale=cd)
                halves(A, "activation", [rrk, "pc"], [rrk], out=rr, in_=rr, func=AF.Exp, scale=cd2)
                halves(A, "activation", [rrk, "consts"], [rrk], out=rr, in_=rr, func=AF.Sqrt, scale=-1.0, bias=epsb[:, 1:2])
                halves(V, "tensor_tensor", [rk[3], rrk], [rk[3]], out=R[3], in0=R[3], in1=rr, op=ALU.mult)
                halves(V, "tensor_tensor", [rk[3], rk[1]], [rk[3]], out=R[3], in0=R[3], in1=R[1], op=ALU.mult)
                if d == 1:
                    V(_I("tensor_tensor", out=R[3][:, T - 16:T], in0=R[3][:, T - 16:T], in1=fl[:, 4:20], op=ALU.mult),
                      reads=[(rk[3], 1), "fl"], writes=[(rk[3], 1)])
                if l == 0 and c == 0:
                    dbgrow(2 + 3 * d, R[4], rk[4])
                    dbgrow(3 + 3 * d, R[3], rk[3])
                seg_scan(rr, R[4], R[3], reverse=(d == 1), okey=rrk, akey=rk[4], bkey=rk[3])
                if l == 0 and c == 0:
                    dbgrow(4 + 3 * d, rr, rrk)
            halves(V, "tensor_tensor", [rk[2], rk[5]], [rk[2]], out=R[2], in0=R[2], in1=R[5], op=ALU.add)
            proj(l, 3 + c, evac_copy(R[3], rk[3]))
            conv_rounds(l, 3)
            gelu_rows(R[3], rk[3], R[4], rk[4])
            halves(V, "tensor_tensor", [rk[2], rk[3]], [rk[2]], out=R[2], in0=R[2], in1=R[3], op=ALU.mult)
            store_y(R[2], rk[2], c)
            halves(A, "activation", [rk[2]], [rk[3]], out=R[3], in_=R[2], func=AF.Square)
            ones_sum(R[3], rk[3], R[6], rk[6], first=(c == 0))
        finalize_branch(l, (0, 1, 2), 384)

    S.dma('pool', _I("dma_start", out=ident_b[:], in_=cst[:, 0, :]), (), ["cst"])
    S.dma('pool', _I("dma_start", out=mblk[:, 0, :], in_=cst[:, 1, :]), (), ["cst"])
    S.dma('pool', _I("dma_start", out=mblk[:, 1, :], in_=cst[:, 2, :]), (), ["cst"])
    S.dma('sp', _I("dma_start", out=bones[:], in_=cst[:, 3, :]), (), ["cst"])
    V(_I("memset", hm[:], 0.0), (), ["hm"])
    V(_I("memset", hm[0:64, 0:1], 1.0), (), ["hm"])
    V(_I("memset", hm[64:128, 1:2], 1.0), (), ["hm"])
    PV_LB = 70

    def hgrn_params(l):
        ppk = ("pp", l % 2)
        A(_I("activation", out=pe_[:], in_=pp[:, l % 2, PV_LB:PV_LB + 24], func=AF.Exp), reads=[ppk], writes=["pe_"])
        for d in range(2):
            for c in range(3):
                e = [pe_[:, (d * 4 + lp) * 3 + c:(d * 4 + lp) * 3 + c + 1] for lp in range(4)]
                j = d * 3 + c
                lb, oml, noml, tmp = (pcb[:, j, i:i + 1] for i in range(4))
                V(_I("tensor_tensor", out=tmp, in0=e[0], in1=e[1], op=ALU.add), ["pe_"], ["pcb"])
                V(_I("tensor_tensor", out=tmp, in0=tmp, in1=e[2], op=ALU.add), ["pe_", "pcb"], ["pcb"])
                V(_I("tensor_tensor", out=tmp, in0=tmp, in1=e[3], op=ALU.add), ["pe_", "pcb"], ["pcb"])
                V(_I("reciprocal", out=tmp, in_=tmp), ["pcb"], ["pcb"])
                if l == 0:
                    V(_I("memset", lb, 0.0), (), ["pcb"])
                else:
                    V(_I("tensor_copy", out=lb, in_=e[1]), ["pe_"], ["pcb"])
                    for lp in range(2, l + 1):
                        V(_I("tensor_tensor", out=lb, in0=lb, in1=e[lp], op=ALU.add), ["pe_", "pcb"], ["pcb"])
                    V(_I("tensor_tensor", out=lb, in0=lb, in1=tmp, op=ALU.mult), ["pcb"], ["pcb"])
                V(_I("tensor_scalar", out=oml, in0=lb, scalar1=-1.0, scalar2=1.0, op0=ALU.mult, op1=ALU.add), ["pcb"], ["pcb"])
                V(_I("tensor_scalar", out=noml, in0=lb, scalar1=1.0, scalar2=-1.0, op0=ALU.mult, op1=ALU.add), ["pcb"], ["pcb"])

    BLK = []
    for s_ in range(2):
        for i_ in range(17):
            BLK.append((s_ * SEG + 128 * i_, 128 if i_ < 16 else 16))

    def chunks_of(bl):
        s_, i_ = divmod(bl, 17)
        if i_ < 16:
            return [(s_ * 33 + 2 * i_, 0, 64), (s_ * 33 + 2 * i_ + 1, 1, 64)]
        return [(s_ * 33 + 32, 0, 16)]

    ps4b = ps[4][:].bitcast(BF16)

    HGPS = ["B4", "B5", "B6", "B7"]

    def hgrn(l):
        S.transfer([("ps", b) for b in range(4, 8)], HGPS)
        PSM_MOD[0] = 4
        hgrn_body(l)
        PSM_MOD[0] = 8
        S.transfer(HGPS, [("ps", b) for b in range(4, 8)])

    def hgrn_body(l):
        hgrn_params(l)
        r1b = rows[1][:].bitcast(BF16)
        r2b = rows[2][:].bitcast(BF16)
        r3b_ = rows[3][:].bitcast(BF16)
        r4b = rows[4][:].bitcast(BF16)
        Sbd = [r1b[:, 0:33 * 128].rearrange("p (g m) -> p g m", m=128), r2b[:, 0:33 * 128].rearrange("p (g m) -> p g m", m=128)]
        scm = r2b[:, 4224:4224 + 1024].rearrange("p (h j m) -> p h j m", h=2, j=4)
        ktb = r2b[:, 5248:5248 + 512].rearrange("p (x j m) -> p x j m", x=2, j=2)
        qt = r4b[:, 0:T]
        k0 = r4b[:, T:2 * T]
        k1 = r3b_[:, 0:T]
        kf = r3b_[:, T:2 * T]
        maskrow = r4b[:, 0:T]
        for c in range(3):
            ppk = ("pp", l % 2)
            slot = load_win(l, 9 + c)
            for bl, (c0, n) in enumerate(BLK):
                b = psm_i[0] % PSM_MOD[0]
                psm_i[0] += 1
                col = 0
                pk = ("ps", b)
                for k in range(8):
                    P(_I("matmul", out=ps[b][0:n, col:col + 128], lhsT=hn[:, k, c0:c0 + n], rhs=winb[:, slot, k, :], start=(k == 0), stop=(k == 7)),
                      reads=[("win", slot), ("hn", c0 // TT), ("hn", (c0 + n - 1) // TT)], writes=[pk])
                A(_I("activation", out=vt[0:n, bl, :], in_=ps[b][0:n, col:col + 128], func=AF.Copy), reads=[pk], writes=["vt"])
            proj(l, 6 + c, evac_copy(R[0], rk[0], AF.Silu))
            for d in range(2):
                if HG_STAGE < 2:
                    continue
                j = d * 3 + c
                lb, oml, noml = (pcb[:, j, i:i + 1] for i in range(3))
                proj(l, 12 + 3 * d + c, evac_copy(R[1], rk[1], AF.Sigmoid))
                A(_I("activation", out=R[2], in_=R[1], func=AF.Ln, scale=oml, bias=lb), reads=[rk[1], "pcb"], writes=[rk[2]])
                A(_I("activation", out=R[1], in_=R[1], func=AF.Identity, scale=noml, bias=oml), reads=[rk[1], "pcb"], writes=[rk[1]])
                G(_I("memset", maskrow, 1.0), (), [rk[4]])
                for s_ in range(2):
                    if d == 0:
                        G(_I("memset", maskrow[:, s_ * SEG:s_ * SEG + 2049:64], 0.0), [rk[4]], [rk[4]])
                    else:
                        G(_I("memset", maskrow[:, s_ * SEG + 63:s_ * SEG + 2048:64], 0.0), [rk[4]], [rk[4]])
                        G(_I("memset", maskrow[:, s_ * SEG + 2063:s_ * SEG + 2064], 0.0), [rk[4]], [rk[4]])
                if d == 0:
                    V(_I("tensor_tensor_scan", out=R[3], data0=maskrow, data1=R[2], initial=0.0, op0=ALU.mult, op1=ALU.add),
                      reads=[rk[4], rk[2]], writes=[rk[3]])
                else:
                    V(_I("tensor_tensor_scan", out=R[3][:, ::-1], data0=maskrow[:, ::-1], data1=R[2][:, ::-1], initial=0.0, op0=ALU.mult, op1=ALU.add),
                      reads=[rk[4], rk[2]], writes=[rk[3]])
                V(_I("tensor_scalar", out=R[3], in0=R[3], scalar1=-80.0, scalar2=None, op0=ALU.max), reads=[rk[3]], writes=[rk[3]])
                A(_I("activation", out=R[2], in_=R[3], func=AF.Exp), reads=[rk[3]], writes=[rk[2]])
                for s_ in range(2):
                    if d == 0:
                        V(_I("tensor_copy", out=dl[:, s_ * 33:s_ * 33 + 32], in_=R[2][:, s_ * SEG + 63:s_ * SEG + 2048:64]), reads=[rk[2]], writes=["dl"])
                        V(_I("tensor_copy", out=dl[:, s_ * 33 + 32:s_ * 33 + 33], in_=R[2][:, s_ * SEG + 2063:s_ * SEG + 2064]), reads=[rk[2]], writes=["dl"])
                    else:
                        V(_I("tensor_copy", out=dl[:, s_ * 33:s_ * 33 + 33], in_=R[2][:, s_ * SEG:s_ * SEG + 2049:64]), reads=[rk[2]], writes=["dl"])
                bc = 32 if d == 0 else 33
                V(_I("tensor_tensor", out=dl[:, bc:bc + 1], in0=dl[:, bc:bc + 1], in1=fl[:, 0:1], op=ALU.mult), reads=["dl", "fl"], writes=["dl"])
                V(_I("tensor_tensor", out=qt, in0=R[0], in1=R[2], op=ALU.mult), reads=[rk[0], rk[2]], writes=[rk[4]])
                A(_I("activation", out=R[2], in_=R[3], func=AF.Exp, scale=-1.0), reads=[rk[3]], writes=[rk[2]])
                V(_I("tensor_tensor", out=kf, in0=R[1], in1=R[2], op=ALU.mult), reads=[rk[1], rk[2]], writes=[rk[3]])
                V(_I("tensor_scalar", out=k0, in0=kf, scalar1=hm[:, 0:1], scalar2=None, op0=ALU.mult), reads=[rk[3], "hm"], writes=[rk[4]])
                V(_I("tensor_scalar", out=k1, in0=kf, scalar1=hm[:, 1:2], scalar2=None, op0=ALU.mult), reads=[rk[3], "hm"], writes=[rk[3]])
                if HG_STAGE < 3:
                    continue
                G(_I("memset", Sbd[0], 0.0), (), [rk[1]])
                G(_I("memset", r2b[:, 0:5248 + 512], 0.0), (), [rk[2]])
                S.transfer([rk[1], rk[2]], ["Sbd", "scm", "ktb"])
                order = list(range(34)) if d == 0 else list(range(33, -1, -1))
                prev_g = None
                zi = 0
                ui = 0
                def prefetch_kt(bi):
                    c0, n = BLK[order[bi]]
                    tb = (4, 6)[bi % 2]
                    tk = "B%d" % tb
                    pstb = ps[tb][:].bitcast(BF16)
                    P(_I("transpose", out=pstb[0:n, 0:128], in_=kf[:, c0:c0 + n], identity=ident_b[:]), reads=[rk[3], "cst"], writes=[tk])
                    kj = bi % 2
                    kk = ("ktb", kj)
                    nl = min(n, 64)
                    A(_I("activation", out=ktb[0:nl, 0, kj, :], in_=pstb[0:nl, 0:128], func=AF.Copy), reads=[tk, "ktb"], writes=[kk])
                    if n == 128:
                        A(_I("activation", out=ktb[64:128, 1, kj, :], in_=pstb[64:128, 0:128], func=AF.Copy), reads=[tk, "ktb"], writes=[kk])
                prefetch_kt(0)
                for bi, bl in enumerate(order):
                    c0, n = BLK[bl]
                    kj = bi % 2
                    kk = ("ktb", kj)
                    if bi + 1 < len(order):
                        prefetch_kt(bi + 1)
                    chs = chunks_of(bl)
                    if d == 1:
                        chs = chs[::-1]
                    for (g, par, C) in chs:
                        ub_ = (5, 7, 2, 3)[ui % 4]
                        uj = 0
                        ui += 1
                        uk = ("B%d" % ub_) if ub_ >= 4 else ("ps", ub_)
                        for h in range(2):
                            P(_I("matmul", out=ps[ub_][64 * h:64 * h + 64, uj * 64:(uj + 1) * 64], lhsT=ktb[0:n, par, kj, 64 * h:64 * h + 64],
                                 rhs=vt[0:n, bl, 64 * h:64 * h + 64], start=True, stop=True),
                              reads=[kk, "vt"], writes=[uk])
                        if prev_g is None:
                            V(_I("tensor_copy", out=Zt[:, zi, :], in_=ps[ub_][:, uj * 64:(uj + 1) * 64]), reads=[uk], writes=[("Z", zi)])
                        else:
                            V(_I("scalar_tensor_tensor", out=Zt[:, 1 - zi, :], in0=Zt[:, zi, :], scalar=dl[:, prev_g:prev_g + 1],
                                 in1=ps[ub_][:, uj * 64:(uj + 1) * 64], op0=ALU.mult, op1=ALU.add),
                              reads=[uk, ("Z", zi), "dl"], writes=[("Z", 1 - zi)])
                            zi = 1 - zi
                        gn = g + 1 if d == 0 else g - 1
                        if 0 <= gn < 66:
                            V(_I("tensor_scalar", out=Sbd[gn // 33][0:64, gn % 33, 0:64], in0=Zt[0:64, zi, :],
                                 scalar1=dl[0:64, g:g + 1], scalar2=None, op0=ALU.mult),
                              reads=[("Z", zi), "dl", "Sbd"], writes=[("Sbd", gn)])
                            V(_I("tensor_scalar", out=Sbd[gn // 33][64:128, gn % 33, 64:128], in0=Zt[64:128, zi, :],
                                 scalar1=dl[64:128, g:g + 1], scalar2=None, op0=ALU.mult),
                              reads=[("Z", zi), "dl", "Sbd"], writes=[("Sbd", gn)])
                        prev_g = g
                def scores_block(bl):
                    c0, n = BLK[bl]
                    sj = bl % 4
                    for h in range(2):
                        sb_ = (6 + h) if bl % 2 == 0 else (2 + h)
                        sk = ("B%d" % sb_) if sb_ >= 4 else ("ps", sb_)
                        kh = k0 if h == 0 else k1
                        P(_I("matmul", out=ps[sb_][0:n, 0:n], lhsT=kh[:, c0:c0 + n], rhs=qt[:, c0:c0 + n], start=True, stop=True),
                          reads=[rk[4], rk[3]], writes=[sk])
                        V(_I("tensor_tensor", out=scm[0:n, h, sj, 0:n], in0=ps[sb_][0:n, 0:n], in1=mblk[0:n, d, 0:n], op=ALU.mult),
                          reads=[sk, "cst", "scm"], writes=[("scm", h, sj)])

                def out_block(bl):
                    c0, n = BLK[bl]
                    s_, i_ = divmod(bl, 17)
                    grp = s_ * 5 + i_ // 4
                    ob = grp % 2
                    ocol0 = c0 - (s_ * SEG + (i_ // 4) * 512)
                    ok = ("ps", ob)
                    sj = bl % 4
                    for h in range(2):
                        P(_I("matmul", out=ps[ob][64 * h:64 * h + 64, ocol0:ocol0 + n], lhsT=vt[0:n, bl, 64 * h:64 * h + 64],
                             rhs=scm[0:n, h, sj, 0:n], start=True, stop=False),
                          reads=[("scm", h, sj), "vt"], writes=[ok])
                    chl = chunks_of(bl)
                    for ci_, (g, par, C) in enumerate(chl):
                        cc0 = c0 + 64 * par
                        oc = ocol0 + 64 * par
                        P(_I("matmul", out=ps[ob][:, oc:oc + C], lhsT=Sbd[g // 33][:, g % 33, :], rhs=qt[:, cc0:cc0 + C], start=False, stop=(ci_ == len(chl) - 1)),
                          reads=[("Sbd", g), "Sbd", rk[4]], writes=[ok])
                    if i_ % 4 == 3 or i_ == 16:
                        g0 = s_ * SEG + (i_ // 4) * 512
                        w_ = c0 + n - g0
                        if d == 0:
                            V(_I("tensor_copy", out=R[5][:, g0:g0 + w_], in_=ps[ob][:, 0:w_]), reads=[ok], writes=[rk[5]])
                        else:
                            V(_I("tensor_tensor", out=R[5][:, g0:g0 + w_], in0=ps[ob][:, 0:w_], in1=R[5][:, g0:g0 + w_], op=ALU.add), reads=[ok, rk[5]], writes=[rk[5]])

                if HG_STAGE >= 4:
                    scores_block(0)
                    for bl in range(34):
                        if bl + 1 < 34:
                            scores_block(bl + 1)
                        out_block(bl)
                S.transfer(["Sbd", "scm", "ktb"] + [("Sbd", g) for g in range(66)] + [("scm", h, j_) for h in range(2) for j_ in range(4)] + [("ktb", j_) for j_ in range(2)],
                           [rk[1], rk[2]])
            if HG_STAGE < 5:
                continue
            A(_I("activation", out=R[0], in_=R[5], func=AF.Square), reads=[rk[5]], writes=[rk[0]])
            ones_sum(R[0], rk[0], R[1], rk[1], first=True, lhs=bones[:], lhs_key="cst")
            rstd_inplace(R[1], rk[1], 64)
            V(_I("tensor_tensor", out=R[5], in0=R[5], in1=R[1], op=ALU.mult), reads=[rk[5], rk[1]], writes=[rk[5]])
            proj(l, 18 + c, evac_copy(R[2], rk[2], AF.Silu))
            V(_I("tensor_tensor", out=R[5], in0=R[5], in1=R[2], op=ALU.mult), reads=[rk[5], rk[2]], writes=[rk[5]])
            store_y(R[5], rk[5], 3 + c)
            A(_I("activation", out=R[0], in_=R[5], func=AF.Square), reads=[rk[5]], writes=[rk[0]])
            ones_sum(R[0], rk[0], R[6], rk[6], first=(c == 0))
        if HG_STAGE < 5:
            zero_branch((3, 4, 5))
            return
        finalize_branch(l, (3, 4, 5), 384)

    S.dma('pool', _I("dma_start", out=idp_b[:], in_=idp[:, :, :]), (), ["cst"])
    S.dma('pool', _I("dma_start", out=tmask[:, 0, :], in_=cst[:, 4, :]), (), ["cst"])
    S.dma('pool', _I("dma_start", out=tmask[:, 1, :], in_=cst[:, 5, :]), (), ["cst"])
    S.dma('sp', _I("dma_start", out=ident_f[:], in_=cst[:, 0, :]), (), ["cst"])
    V(_I("tensor_scalar", out=nhm[:], in0=hm[:], scalar1=-1.0, scalar2=None, op0=ALU.mult), ["hm"], ["hm"])
    PV_SD, PV_GB = 94, 96
    hnb = hn[:].rearrange("p k t -> p (k t)")
    hnf = hnb.bitcast(F32)
    Mt = hnb[:, 0:4096].rearrange("p (g m) -> p g m", m=128)
    Cmre = hnb[:, 4096:8192].rearrange("p (g m) -> p g m", m=128)
    Cmim = hnb[:, 8192:12288].rearrange("p (g m) -> p g m", m=128)
    Wtre = hnb[:, 12288:14336].rearrange("p (g m) -> p g m", m=128)
    Wtim = hnb[:, 14336:16384].rearrange("p (g m) -> p g m", m=128)
    u8s = hnb[:, 16384:20512].rearrange("p (g n) -> p g n", n=516)
    y8s = hnb[:, 20512:24640].rearrange("p (g n) -> p g n", n=516)
    Hp = hnb[:, 24640:26704].rearrange("p (g n) -> p g n", n=516)
    HB_ = [[hnf[:, 13352 + 516 * (2 * x + c_):13352 + 516 * (2 * x + c_ + 1)] for c_ in range(2)] for x in range(2)]
    TMP = hnf[:, 8192:8192 + 2048]
    WM = hnf[:, 10240:10240 + 256].rearrange("p (x m) -> p x m", x=2)
    MUL, ADD, SUB = ALU.mult, ALU.add, ALU.subtract

    def tt(out, a, b, op, r, w):
        V(_I("tensor_tensor", out=out, in0=a, in1=b, op=op), r, w)

    def ts(out, a, s1, op0, r, w, s2=None, op1=None):
        if op1 is None:
            V(_I("tensor_scalar", out=out, in0=a, scalar1=s1, scalar2=None, op0=op0), r, w)
        else:
            V(_I("tensor_scalar", out=out, in0=a, scalar1=s1, scalar2=s2, op0=op0, op1=op1), r, w)

    def stt(out, a, sc, b, op0, op1, r, w):
        V(_I("scalar_tensor_tensor", out=out, in0=a, scalar=sc, in1=b, op0=op0, op1=op1), r, w)

    r6 = R[6]
    Ak_re = r6[:, 3600:3760].rearrange("p (t k) -> p t k", k=10)
    Ak_im = r6[:, 3760:3920].rearrange("p (t k) -> p t k", k=10)
    nAk_im = r6[:, 3920:4080].rearrange("p (t k) -> p t k", k=10)
    rr8 = r6[:, 1024 + 16 * 20:1040 + 16 * 20]
    EB = [[hnf[:, 15416 + 516 * c_:15416 + 516 * (c_ + 1)] for c_ in range(2)],
          [hnf[:, 10256 + 516 * c_:10256 + 516 * (c_ + 1)] for c_ in range(2)]]

    def s5_prep(l):
        TK = ["s5tab"]

        def t16(i):
            return r6[:, 1024 + 16 * i:1040 + 16 * i]
        Bt = r6[:, 0:512].rearrange("p (t c h) -> p t c h", t=16, c=2)
        Ct = r6[:, 512:1024].rearrange("p (t c h) -> p t c h", t=16, c=2)
        S.dma('sp', _I("dma_start", out=s5ps[:], in_=s5p[l]), (), ["s5ps"])
        S.dma('sp', _I("dma_start", out=Bt, in_=s5B[l]), (), TK)
        S.dma('sp', _I("dma_start", out=Ct, in_=s5C[l]), (), TK)
        are0, aim0, ldt = s5ps[:, :, 0], s5ps[:, :, 1], s5ps[:, :, 2]
        (dlt, xr, xi, m16, th, t2, acc, sinv, cosv, re, im, ta, tb, i8r, i8i, am1, nr, ni, gre, gim) = (t16(i) for i in range(20))
        RK = TK + ["s5ps"]
        A(_I("activation", out=dlt, in_=ldt, func=AF.Exp), RK, TK)
        tt(xr, are0, dlt, MUL, RK, TK)
        tt(xi, aim0, dlt, MUL, RK, TK)
        A(_I("activation", out=m16, in_=xr, func=AF.Exp, scale=1.0 / 16), TK, TK)
        ts(th, xi, 1.0 / 16, MUL, TK, TK)
        tt(t2, th, th, MUL, TK, TK)
        ts(acc, t2, -1.0 / 39916800, MUL, TK, TK)
        for c_ in (1.0 / 362880, -1.0 / 5040, 1.0 / 120, -1.0 / 6):
            stt(acc, acc, c_, t2, ADD, MUL, TK, TK)
        stt(sinv, acc, 1.0, th, ADD, MUL, TK, TK)
        ts(acc, t2, 1.0 / 479001600, MUL, TK, TK)
        for c_ in (-1.0 / 3628800, 1.0 / 40320, -1.0 / 720, 1.0 / 24, -0.5):
            stt(acc, acc, c_, t2, ADD, MUL, TK, TK)
        ts(cosv, acc, 1.0, ADD, TK, TK)
        tt(re, m16, cosv, MUL, TK, TK)
        tt(im, m16, sinv, MUL, TK, TK)
        for _ in range(4):
            tt(ta, re, re, MUL, TK, TK)
            tt(tb, im, im, MUL, TK, TK)
            stt(im, re, 2.0, im, MUL, MUL, TK, TK)
            tt(re, ta, tb, SUB, TK, TK)
        Pwr = r6[:, 2048:2192].rearrange("p (t m) -> p t m", m=9)
        Pwi = r6[:, 2200:2344].rearrange("p (t m) -> p t m", m=9)
        V(_I("memset", Pwr[:, :, 0], 1.0), (), TK)
        V(_I("memset", Pwi[:, :, 0], 0.0), (), TK)
        V(_I("tensor_copy", out=Pwr[:, :, 1], in_=re), TK, TK)
        V(_I("tensor_copy", out=Pwi[:, :, 1], in_=im), TK, TK)
        for m in range(1, 8):
            tt(ta, Pwr[:, :, m], re, MUL, TK, TK)
            tt(tb, Pwi[:, :, m], im, MUL, TK, TK)
            tt(Pwr[:, :, m + 1], ta, tb, SUB, TK, TK)
            tt(ta, Pwr[:, :, m], im, MUL, TK, TK)
            tt(tb, Pwi[:, :, m], re, MUL, TK, TK)
            tt(Pwi[:, :, m + 1], ta, tb, ADD, TK, TK)
        tt(ta, Pwr[:, :, 8], Pwr[:, :, 8], MUL, TK, TK)
        tt(tb, Pwi[:, :, 8], Pwi[:, :, 8], MUL, TK, TK)
        tt(ta, ta, tb, ADD, TK, TK)
        V(_I("reciprocal", out=ta, in_=ta), TK, TK)
        tt(i8r, Pwr[:, :, 8], ta, MUL, TK, TK)
        stt(i8i, Pwi[:, :, 8], -1.0, ta, MUL, MUL, TK, TK)
        rinv = t16(21)
        A(_I("activation", out=rinv, in_=ta, func=AF.Sqrt), TK, TK)
        V(_I("reciprocal", out=rr8, in_=rinv), TK, TK)
        tt(Ak_re[:, :, 0], Pwr[:, :, 8], rinv, MUL, TK, TK)
        stt(Ak_im[:, :, 0], Pwi[:, :, 8], -1.0, rinv, MUL, MUL, TK, TK)
        for k in range(9):
            tt(ta, Ak_re[:, :, k], Ak_re[:, :, k], MUL, TK, TK)
            tt(tb, Ak_im[:, :, k], Ak_im[:, :, k], MUL, TK, TK)
            tt(Ak_re[:, :, k + 1], ta, tb, SUB, TK, TK)
            stt(Ak_im[:, :, k + 1], Ak_re[:, :, k], 2.0, Ak_im[:, :, k], MUL, MUL, TK, TK)
        ts(nAk_im, Ak_im, -1.0, MUL, TK, TK)
        ts(am1, re, -1.0, ADD, TK, TK)
        tt(ta, am1, are0, MUL, RK, TK)
        tt(tb, im, aim0, MUL, RK, TK)
        tt(nr, ta, tb, ADD, TK, TK)
        tt(ta, im, are0, MUL, RK, TK)
        tt(tb, am1, aim0, MUL, RK, TK)
        tt(ni, ta, tb, SUB, TK, TK)
        tt(ta, are0, are0, MUL, RK, TK)
        tt(tb, aim0, aim0, MUL, RK, TK)
        tt(ta, ta, tb, ADD, TK, TK)
        V(_I("reciprocal", out=ta, in_=ta), TK, TK)
        tt(gre, nr, ta, MUL, TK, TK)
        tt(gim, ni, ta, MUL, TK, TK)
        bre = r6[:, 3000:3256].rearrange("p (t h) -> p t h", h=16)
        bim = r6[:, 3256:3512].rearrange("p (t h) -> p t h", h=16)
        tmp3 = TMP[:, 0:256].rearrange("p (t h) -> p t h", h=16)
        TT_ = ["s5tmp"]
        gre_b = gre.unsqueeze(2).broadcast_to([128, 16, 16])
        gim_b = gim.unsqueeze(2).broadcast_to([128, 16, 16])
        tt(bre, gre_b, Bt[:, :, 0, :], MUL, TK, TK)
        tt(tmp3, gim_b, Bt[:, :, 1, :], MUL, TK, TT_)
        tt(bre, bre, tmp3, SUB, TK + TT_, TK)
        tt(bim, gre_b, Bt[:, :, 1, :], MUL, TK, TK)
        tt(tmp3, gim_b, Bt[:, :, 0, :], MUL, TK, TT_)
        tt(bim, bim, tmp3, ADD, TK + TT_, TK)
        PwWr = r6[:, 2400:2528].rearrange("p (t j) -> p t j", j=8)
        PwWi = r6[:, 2528:2656].rearrange("p (t j) -> p t j", j=8)
        PwCr = r6[:, 2656:2784].rearrange("p (t j) -> p t j", j=8)
        PwCi = r6[:, 2784:2912].rearrange("p (t j) -> p t j", j=8)
        for (dst, src) in ((PwWr, Pwr), (PwWi, Pwi)):
            V(_I("tensor_copy", out=dst[:, 0:8, :], in_=src[:, 0:8, 7::-1]), TK, TK)
            V(_I("tensor_copy", out=dst[:, 8:16, :], in_=src[:, 8:16, 0:8]), TK, TK)
        for (dst, src) in ((PwCr, Pwr), (PwCi, Pwi)):
            V(_I("tensor_copy", out=dst[:, 0:8, :], in_=src[:, 0:8, 1:9]), TK, TK)
            V(_I("tensor_copy", out=dst[:, 8:16, :], in_=src[:, 8:16, 8:0:-1]), TK, TK)
        if S5_STAGE < 3:
            return
        sh4 = [128, 16, 8, 16]
        Wre = R[3][:, 0:2048].rearrange("p (t j h) -> p t j h", t=16, j=8)
        Wim = R[3][:, 2048:4096].rearrange("p (t j h) -> p t j h", t=16, j=8)
        Cre = R[4][:, 0:2048].rearrange("p (t j h) -> p t j h", t=16, j=8)
        Cim = R[4][:, 2048:4096].rearrange("p (t j h) -> p t j h", t=16, j=8)
        Ypr = R[5][:, 0:2048].rearrange("p (t j h) -> p t j h", t=16, j=8)
        Ypi = R[5][:, 2048:4096].rearrange("p (t j h) -> p t j h", t=16, j=8)
        tmp4 = TMP.rearrange("p (t j h) -> p t j h", t=16, j=8)

        def cmul(ore, oim, ar, ai, br_, bi_, okey):
            tt(ore, ar, br_, MUL, TK + [okey], [okey])
            tt(tmp4, ai, bi_, MUL, TK + [okey], TT_)
            tt(ore, ore, tmp4, SUB, [okey] + TT_, [okey])
            tt(oim, ar, bi_, MUL, TK + [okey], [okey])
            tt(tmp4, ai, br_, MUL, TK + [okey], TT_)
            tt(oim, oim, tmp4, ADD, [okey] + TT_, [okey])
        cmul(Wre, Wim, bre.unsqueeze(2).broadcast_to(sh4), bim.unsqueeze(2).broadcast_to(sh4),
             PwWr.unsqueeze(3).broadcast_to(sh4), PwWi.unsqueeze(3).broadcast_to(sh4), rk[3])
        cmul(Cre, Cim, Ct[:, :, 0, :].unsqueeze(2).broadcast_to(sh4), Ct[:, :, 1, :].unsqueeze(2).broadcast_to(sh4),
             PwCr.unsqueeze(3).broadcast_to(sh4), PwCi.unsqueeze(3).broadcast_to(sh4), rk[4])
        i8r4 = i8r.unsqueeze(2).unsqueeze(3).broadcast_to(sh4)
        i8i4 = i8i.unsqueeze(2).unsqueeze(3).broadcast_to(sh4)
        tt(Ypr, Cre, i8r4, MUL, TK + [rk[4]], [rk[5]])
        tt(tmp4, Cim, i8i4, MUL, TK + [rk[4]], TT_)
        tt(Ypr, Ypr, tmp4, SUB, [rk[5]] + TT_, [rk[5]])
        tt(Ypi, Cre, i8i4, MUL, TK + [rk[4]], [rk[5]])
        tt(tmp4, Cim, i8r4, MUL, TK + [rk[4]], TT_)
        tt(Ypi, Ypi, tmp4, ADD, [rk[5]] + TT_, [rk[5]])
        W2 = [R[3][:, 0:2048], R[3][:, 2048:4096]]
        C2 = [R[4][:, 0:2048], R[4][:, 2048:4096]]
        Y2 = [R[5][:, 0:2048], R[5][:, 2048:4096]]
        if S5_STAGE < 4:
            return
        for tile in range(16):
            for comp, dstW in ((0, Wtre), (1, Wtim)):
                b = psm_i[0] % PSM_MOD[0]
                psm_i[0] += 1
                pk = ("ps", b)
                P(_I("transpose", out=ps[b][:, 0:128], in_=W2[comp][:, tile * 128:(tile + 1) * 128], identity=ident_f[:]), [rk[3], "cst"], [pk])
                A(_I("activation", out=dstW[:, tile, :], in_=ps[b][:, 0:128], func=AF.Copy), [pk], ["Wt"])
        for d in range(2):
            for g in range(16):
                tile = d * 8 + g // 2
                gp = g % 2
                idx = d * 16 + g
                sl = slice(tile * 128, (tile + 1) * 128)
                ts(Cmre[:, idx, :], C2[0][:, sl], hm[:, gp:gp + 1], MUL, [rk[4], "hm"], ["Cm"])
                ts(Cmim[:, idx, :], C2[1][:, sl], nhm[:, gp:gp + 1], MUL, [rk[4], "hm"], ["Cm"])
                ts(WM[:, 0, :], W2[0][:, sl], hm[:, gp:gp + 1], MUL, [rk[3], "hm"], ["WM"])
                ts(WM[:, 1, :], W2[1][:, sl], nhm[:, gp:gp + 1], MUL, [rk[3], "hm"], ["WM"])
                b = psm_i[0] % PSM_MOD[0]
                psm_i[0] += 1
                pk = ("ps", b)
                P(_I("matmul", out=ps[b][:, 0:128], lhsT=WM[:, 0, :], rhs=Y2[0][:, sl], start=True, stop=False), ["WM", rk[5]], [pk])
                P(_I("matmul", out=ps[b][:, 0:128], lhsT=WM[:, 1, :], rhs=Y2[1][:, sl], start=False, stop=True), ["WM", rk[5]], [pk])
                tt(Mt[:, idx, :], ps[b][:, 0:128], tmask[:, d, :], MUL, [pk, "cst"], ["Mt"])

    def hs_step(HBt, hkey, seg, d, tile, k, cur):
        c0 = seg * 258
        sh = 1 << k
        nn = 258 - sh
        if d == 0:
            dsl, ssl, rsl = slice(c0 + sh, c0 + 258), slice(c0, c0 + nn), slice(c0, c0 + sh)
        else:
            dsl, ssl, rsl = slice(c0, c0 + nn), slice(c0 + sh, c0 + 258), slice(c0 + nn, c0 + 258)
        o_, n_ = HBt[cur], HBt[1 - cur]
        pr = Ak_re[:, tile, k:k + 1]
        pi = Ak_im[:, tile, k:k + 1]
        npi = nAk_im[:, tile, k:k + 1]
        ok_, nk_ = (hkey, cur), (hkey, 1 - cur)
        stt(n_[0][:, dsl], o_[0][:, ssl], pr, o_[0][:, dsl], MUL, ADD, [ok_, "s5tab"], [nk_])
        stt(n_[0][:, dsl], o_[1][:, ssl], npi, n_[0][:, dsl], MUL, ADD, [ok_, nk_, "s5tab"], [nk_])
        stt(n_[1][:, dsl], o_[1][:, ssl], pr, o_[1][:, dsl], MUL, ADD, [ok_, "s5tab"], [nk_])
        stt(n_[1][:, dsl], o_[0][:, ssl], pi, n_[1][:, dsl], MUL, ADD, [ok_, nk_, "s5tab"], [nk_])
        A(_I("activation", out=n_[0][:, rsl], in_=o_[0][:, rsl], func=AF.Copy), [ok_], [nk_])
        A(_I("activation", out=n_[1][:, rsl], in_=o_[1][:, rsl], func=AF.Copy), [ok_], [nk_])

    def s5_main(l):
        ppk = ("pp", l % 2)
        ubv = [rows[2][:].bitcast(BF16)[:, 0:T], rows[2][:].bitcast(BF16)[:, T:2 * T]]
        for cc in range(2):
            for gq in range(8):
                for s_ in range(2):
                    b = psm_i[0] % PSM_MOD[0]
                    psm_i[0] += 1
                    pk = ("ps", b)
                    for j in range(8):
                        P(_I("matmul", out=ps[b][:, 0:258], lhsT=idp_b[:, gq, (7 - j) * 16:(7 - j) * 16 + 128],
                             rhs=ubv[cc][:, s_ * SEG + j:s_ * SEG + SEG:8], start=(j == 0), stop=(j == 7)),
                          [rk[2], "cst"], [pk])
                    A(_I("activation", out=u8s[:, gq, s_ * 258:(s_ + 1) * 258], in_=ps[b][:, 0:258], func=AF.Copy), [pk], [("u8s", gq)])
            HS = []
            for tl in range(4):
                base = rows[4 + tl // 2][:]
                o0 = (tl % 2) * 2064
                HS.append([[base[:, o0 + 516 * (2 * x + c_):o0 + 516 * (2 * x + c_ + 1)] for c_ in range(2)] for x in range(2)])
            Hp4 = rows[3][:].bitcast(BF16).rearrange("p (t g n) -> p t g n", t=4, g=4)
            S.transfer([rk[4], rk[5], rk[3]], [(("H", tl), x) for tl in range(4) for x in range(2)] + [("Hp", tl, g_) for tl in range(4) for g_ in range(4)])
            S.transfer([("y8s", g_) for g_ in range(8)] + ["s5tmp", "WM"], [("E", 0), ("E", 1)])
            for d in range(2):
                for tl in range(4):
                    tile = d * 8 + cc * 4 + tl
                    for comp, Wt_ in ((0, Wtre), (1, Wtim)):
                        for s_ in range(2):
                            b = psm_i[0] % PSM_MOD[0]
                            psm_i[0] += 1
                            pk = ("ps", b)
                            for gp in range(2):
                                P(_I("matmul", out=ps[b][64 * gp:64 * gp + 64, 0:258], lhsT=Wt_[:, tile, 64 * gp:64 * gp + 64],
                                     rhs=u8s[:, 2 * tl + gp, s_ * 258:(s_ + 1) * 258], start=True, stop=True),
                                  ["Wt", ("u8s", 2 * tl + gp)], [pk])
                            A(_I("activation", out=HS[tl][0][comp][:, s_ * 258:(s_ + 1) * 258], in_=ps[b][:, 0:258], func=AF.Copy), [pk], [(("H", tl), 0)])
                for tl in range(4):
                    tile = d * 8 + cc * 4 + tl
                    Er, Ei = EB[tl % 2]
                    ek = ("E", tl % 2)
                    h0k, h1k = (("H", tl), 0), (("H", tl), 1)
                    D_re, D_im = HS[tl][0]
                    X_re, X_im = HS[tl][1]
                    V(_I("memset", Er[:, 0:1], 1.0), (), [ek])
                    V(_I("memset", Ei[:, 0:1], 0.0), (), [ek])
                    for k in range(10):
                        sh = 1 << k
                        nn = min(sh, 516 - sh)
                        src, dst = slice(0, nn), slice(sh, sh + nn)
                        wr, wi, nwi = Ak_re[:, tile, k:k + 1], Ak_im[:, tile, k:k + 1], nAk_im[:, tile, k:k + 1]
                        ts(Er[:, dst], Er[:, src], wr, MUL, [ek, "s5tab"], [ek])
                        stt(Er[:, dst], Ei[:, src], nwi, Er[:, dst], MUL, ADD, [ek, "s5tab"], [ek])
                        ts(Ei[:, dst], Ei[:, src], wr, MUL, [ek, "s5tab"], [ek])
                        stt(Ei[:, dst], Er[:, src], wi, Ei[:, dst], MUL, ADD, [ek, "s5tab"], [ek])
                    Fr = Er if d == 0 else Er[:, ::-1]
                    Fi = Ei if d == 0 else Ei[:, ::-1]
                    tt(X_re, D_re, Fr, MUL, [h0k, ek], [h1k])
                    tt(X_im, D_re, Fi, MUL, [h0k, ek], [h1k])
                    tt(D_re, D_im, Fi, MUL, [h0k, ek], [h0k])
                    tt(X_re, X_re, D_re, SUB, [h0k, h1k], [h1k])
                    tt(D_re, D_im, Fr, MUL, [h0k, ek], [h0k])
                    tt(X_im, X_im, D_re, ADD, [h0k, h1k], [h1k])
                    rb = rr8[:, tile:tile + 1].broadcast_to([128, 258])
                    first, second = (0, 1) if d == 0 else (1, 0)
                    for comp in range(2):
                        Xc, Gc = HS[tl][1][comp], HS[tl][0][comp]
                        ck = ("cf", tl, comp)
                        cfc = cf[:, 2 * tl + comp:2 * tl + comp + 1]
                        for n_, sg in enumerate((first, second)):
                            sl = slice(sg * 258, (sg + 1) * 258)
                            xo, go = Xc[:, sl], Gc[:, sl]
                            if d == 1:
                                xo, go = xo[:, ::-1], go[:, ::-1]
                            if n_ == 0:
                                V(_I("tensor_tensor_scan", out=go, data0=rb, data1=xo, initial=0.0, op0=MUL, op1=ADD), [h1k, "s5tab"], [h0k])
                                col = 257 if d == 0 else 258
                                tt(cfc, Gc[:, col:col + 1], fl[:, 0:1], MUL, [h0k, "fl"], [ck])
                            else:
                                V(_I("tensor_tensor_scan", out=go, data0=rb, data1=xo, initial=cfc, op0=MUL, op1=ADD), [h1k, "s5tab", ck], [h0k])
                    G_re, G_im = HS[tl][0]
                    T1, T2 = HS[tl][1]
                    osl, isl = (slice(1, 516), slice(0, 515)) if d == 0 else (slice(0, 515), slice(1, 516))
                    for comp in range(2):
                        hp = Hp4[:, tl, d * 2 + comp, :]
                        hkk = ("Hp", tl, d * 2 + comp)
                        if comp == 0:
                            tt(T1, G_re, Fr, MUL, [h0k, ek], [h1k])
                            tt(T2, G_im, Fi, MUL, [h0k, ek], [h1k])
                            tt(hp[:, osl], T1[:, isl], T2[:, isl], ADD, [h1k], [hkk])
                        else:
                            tt(T1, G_im, Fr, MUL, [h0k, ek], [h1k])
                            tt(T2, G_re, Fi, MUL, [h0k, ek], [h1k])
                            tt(hp[:, osl], T1[:, isl], T2[:, isl], SUB, [h1k], [hkk])
                        if d == 0:
                            V(_I("memset", hp[:, 0:1], 0.0), (), [hkk])
                            tt(hp[:, 258:259], hp[:, 258:259], fl[:, 0:1], MUL, [hkk, "fl"], [hkk])
                        else:
                            V(_I("memset", hp[:, 515:516], 0.0), (), [hkk])
                            tt(hp[:, 257:258], hp[:, 257:258], fl[:, 0:1], MUL, [hkk, "fl"], [hkk])
            S.transfer([("E", 0), ("E", 1)], [("y8s", g_) for g_ in range(8)] + ["s5tmp"])
            for tl in range(4):
                for gp in range(2):
                    gq = 2 * tl + gp
                    for s_ in range(2):
                        b = psm_i[0] % PSM_MOD[0]
                        psm_i[0] += 1
                        pk = ("ps", b)
                        hsl = slice(s_ * 258, (s_ + 1) * 258)
                        n_mm = 0
                        for d in range(2):
                            idx = d * 16 + cc * 8 + gq
                            for lhs, rhs, rkeys in ((Mt[:, idx, :], u8s[:, gq, hsl], ["Mt", ("u8s", gq)]),
                                                    (Cmre[:, idx, :], Hp4[:, tl, d * 2, hsl], ["Cm", ("Hp", tl, d * 2)]),
                                                    (Cmim[:, idx, :], Hp4[:, tl, d * 2 + 1, hsl], ["Cm", ("Hp", tl, d * 2 + 1)])):
                                P(_I("matmul", out=ps[b][:, 0:258], lhsT=lhs, rhs=rhs, start=(n_mm == 0), stop=(n_mm == 5)), rkeys, [pk])
                                n_mm += 1
                        A(_I("activation", out=y8s[:, gq, hsl], in_=ps[b][:, 0:258], func=AF.Copy), [pk], [("y8s", gq)])
            S.transfer([(("H", tl), x) for tl in range(4) for x in range(2)] + [("Hp", tl, g_) for tl in range(4) for g_ in range(4)] + [("cf", tl, c_) for tl in range(4) for c_ in range(2)],
                       [rk[4], rk[5], rk[3]])

            for j in range(8):
                for s_ in range(2):
                    b = psm_i[0] % PSM_MOD[0]
                    psm_i[0] += 1
                    pk = ("ps", b)
                    for gq in range(8):
                        P(_I("matmul", out=ps[b][:, 0:258], lhsT=idp_b[:, j, (7 - gq) * 16:(7 - gq) * 16 + 128],
                             rhs=y8s[:, gq, s_ * 258:(s_ + 1) * 258], start=(gq == 0), stop=(gq == 7)),
                          [("y8s", gq), "cst"], [pk])
                    V(_I("tensor_copy", out=R[3][:, s_ * SEG + j * 258:s_ * SEG + (j + 1) * 258], in_=ps[b][:, 0:258]), [pk], [rk[3]])
            for s_ in range(2):
                nat = R[cc][:, s_ * SEG:(s_ + 1) * SEG].rearrange("p (n j) -> p n j", j=8)
                perm = R[3][:, s_ * SEG:(s_ + 1) * SEG].rearrange("p (j n) -> p n j", j=8)
                stt(nat, nat, ppc(l, PV_SD + cc), perm, MUL, ADD, [rk[cc], rk[3], ppk], [rk[cc]])
            gelu_rows(R[cc], rk[cc], R[4], rk[4])
            A(_I("activation", out=ubv[cc], in_=R[cc], func=AF.Copy), [rk[cc]], [rk[2]])
        S.transfer(["s5tab"], [rk[6]])
        S.dma('pool', _I("dma_start", out=glw[:], in_=glu_w[l].rearrange("(k p) m -> p k m", p=128)), (), ["glw"])
        for m in range(2):
            for i in range(NMT):
                b = psm_i[0] % PSM_MOD[0]
                psm_i[0] += 1
                pk = ("ps", b)
                for kc in range(2):
                    P(_I("matmul", out=ps[b][:, 0:MT], lhsT=glw[:, kc, m * 128:(m + 1) * 128], rhs=ubv[kc][:, i * MT:(i + 1) * MT],
                         start=(kc == 0), stop=(kc == 1)), ["glw", rk[2]], [pk])
                A(_I("activation", out=R[5][:, i * MT:(i + 1) * MT], in_=ps[b][:, 0:MT], func=AF.Sigmoid, bias=ppc(l, PV_GB + m), scale=1.0),
                  [pk, ppk], [rk[5]])
            tt(R[m], R[m], R[5], MUL, [rk[m], rk[5]], [rk[m]])
            store_y(R[m], rk[m], 6 + m)
            A(_I("activation", out=R[4], in_=R[m], func=AF.Square), [rk[m]], [rk[4]])
            ones_sum(R[4], rk[4], R[6], rk[6], first=(m == 0))
        finalize_branch(l, (6, 7), 256)

    S5K = ["Mt", "Cm", "Wt", "WM", "s5tmp", ("H", 0), ("H", 1)] + [("u8s", g) for g in range(8)] + [("y8s", g) for g in range(8)] + [("Hp", g) for g in range(4)]
    HNK = [("hn", i) for i in range(NTT)]

    def s5(l):
        for cc in range(2):
            def ev(i, psap, c0, pk, cc=cc):
                A(_I("activation", out=R[cc][:, c0:c0 + MT], in_=psap, func=AF.Copy), [pk], [rk[cc]])
                V(_I("tensor_copy", out=rows[2][:].bitcast(BF16)[:, cc * T + c0:cc * T + c0 + MT], in_=R[cc][:, c0:c0 + MT]), [rk[cc]], [rk[2]])
            proj(l, 21 + cc, ev)
        S.transfer(HNK, S5K)
        S.transfer([rk[6]], ["s5tab"])
        if S5_STAGE >= 2:
            s5_prep(l)
        if S5_STAGE >= 5:
            s5_main_wrap(l)
        else:
            S.transfer(S5K, HNK)
            S.transfer(["s5tab"], [rk[6]])
            zero_branch((6, 7))

    def s5_main_wrap(l):
        s5_main_pre(l)

    def s5_main_pre(l):
        s5_main(l)
        S.transfer(S5K, HNK)

    TTK = ["h_t", "u_t", "ym_t", "hn2_t", "sq_t", "rstd_t", ("wp", 0), ("wp", 1)] + [("relu", j) for j in range(4)]
    load_params(0)
    layer0_norm()
    for l in range(NLAYERS):
        S.transfer(TTK, rk)
        if l + 1 < DEPTH:
            load_params(l + 1)
        if ENABLE_A:
            rglru(l)
        else:
            zero_branch((0, 1, 2))
        conv_rounds(l, 18)
        if ENABLE_B:
            hgrn(l)
        else:
            zero_branch((3, 4, 5))
        if ENABLE_C:
            s5(l)
        else:
            zero_branch((6, 7))
        S.transfer(rk, TTK)
        tt_phase(l, True)
    S.emit()
    es.close()
    return nc


def make_consts():
    c = np.zeros((128, 6, 128), np.float32)
    i = np.arange(128)
    jj = i // 16
    c[:, 4, :] = jj[:, None] <= jj[None, :]
    c[:, 5, :] = jj[:, None] >= jj[None, :]
    c[:, 0, :] = np.eye(128)
    same = (i[:, None] // 64) == (i[None, :] // 64)
    c[:, 1, :] = same & (i[None, :] >= i[:, None])
    c[:, 2, :] = same & (i[None, :] <= i[:, None])
    c[:, 3, :] = same
    return c


def pack_s5(inp):
    def lay(x):
        l_, d_, g_, p_ = x.shape[:4]
        rest = x.shape[4:]
        y = x.reshape(l_, d_, g_ // 2, 2, p_, *rest)
        y = np.moveaxis(y, (3, 4), (1, 2))
        return np.ascontiguousarray(y.reshape(l_, 2 * p_, d_ * (g_ // 2), *rest))
    a_re, a_im = inp["s5_a_re"], inp["s5_a_im"]
    ldt = np.broadcast_to(inp["s5_log_dt"][..., None], a_re.shape)
    s5p = lay(np.stack([a_re, a_im, ldt], axis=-1))
    s5B = lay(np.stack([inp["s5_b_re"], inp["s5_b_im"]], axis=-2))
    ct = np.stack([np.swapaxes(inp["s5_c_re"], -1, -2), np.swapaxes(inp["s5_c_im"], -1, -2)], axis=-2)
    s5C = lay(ct)
    idp = np.zeros((128, 8, 240), np.float32)
    r = np.arange(128)
    idp[r, r // 16, 112 + r % 16] = 1.0
    return {"s5p": s5p.astype(np.float32), "s5B": s5B.astype(np.float32), "s5C": s5C.astype(np.float32), "idp": idp}


def pack_pvec(inp):
    pv = np.zeros((DEPTH, 128, NPV), np.float32)
    for l in range(DEPTH):
        pv[l, :, 0:8] = inp["norm_mix"][l].reshape(8, 128).T
        pv[l, :, 8:16] = inp["norm_mlp"][l].reshape(8, 128).T
        pv[l, :, 16:24] = inp["mix_gain"][l].reshape(8, 128).T
        pv[l, :, 24:32] = inp["norm_final"].reshape(8, 128).T
        for c in range(3):
            sl = slice(c * 128, (c + 1) * 128)
            base = 32 + c * 11
            for j in range(4):
                pv[l, :, base + j] = inp["conv_w"][l, j, sl]
            pv[l, :, base + 4] = inp["conv_b"][l, sl]
            pv[l, :, base + 5] = inp["rg_br"][l, 0, sl]
            pv[l, :, base + 6] = inp["rg_br"][l, 1, sl]
            pv[l, :, base + 7] = inp["rg_bi"][l, 0, sl]
            pv[l, :, base + 8] = inp["rg_bi"][l, 1, sl]
            pv[l, :, base + 9] = inp["rg_lambda"][l, 0, sl]
            pv[l, :, base + 10] = inp["rg_lambda"][l, 1, sl]
        for d in range(2):
            for lp in range(4):
                for c in range(3):
                    pv[l, :, 70 + (d * 4 + lp) * 3 + c] = inp["hgrn_lb_logits"][d, lp, c * 128:(c + 1) * 128]
        for cc in range(2):
            pv[l, :, 94 + cc] = inp["s5_d"][l, cc * 128:(cc + 1) * 128]
            pv[l, :, 96 + cc] = inp["s5_glu_b"][l, cc * 128:(cc + 1) * 128]
    return pv


_NC_CACHE = {}


def kernel(**inputs):
    inp = {k: np.asarray(v) for k, v in inputs.items()}
    xp, xs, meta = inp["x_prompt"], inp["x_sample"], inp["meta_tokens"]
    if "nc" not in _NC_CACHE:
        _NC_CACHE["nc"] = build_program()
    nc = _NC_CACHE["nc"]
    pv = pack_pvec(inp)
    rg_w = np.ascontiguousarray(np.stack([inp["rg_wr"], inp["rg_wi"]], axis=1))
    shared = {
        "pvec": pv, "w_in": inp["w_in"], "w_out": inp["w_out"], "w_up": inp["w_up"], "w_down": inp["w_down"], "rg_w": rg_w, "cst": make_consts(), "glu_w": inp["s5_glu_w"],
    }
    shared.update(pack_s5(inp))
    in_maps = []
    for c in range(8):
        xT = np.zeros((D, T), np.float32)
        fl = np.zeros((128, 20), np.float32)
        if c < 4:
            xT[:, 0:16] = meta.T
            xT[:, 16:16 + 4096] = xp[c].T
            fl[:, 0] = 1.0
            fl[:, 4:20] = 0.0
        else:
            for s in range(2):
                xT[:, s * SEG:s * SEG + 16] = meta.T
                xT[:, s * SEG + 16:(s + 1) * SEG] = xs[2 * (c - 4) + s].T
            fl[:, 0] = 0.0
            fl[:, 4:20] = 1.0
        fl[:, 1] = 1.0 - fl[:, 0]
        m = {"xT": xT, "flags": fl}
        m.update(shared)
        in_maps.append(m)
    res = run_bass_kernel_spmd(nc, in_maps, core_ids=list(range(8)))
    if DEBUG:
        DBG_OUT["dbg"] = [np.asarray(res.results[c]["dbg"]) for c in range(8)]
    yp = np.zeros((4, 4096, D), np.float32)
    ys = np.zeros((8, 2048, D), np.float32)
    for c in range(8):
        yT = np.asarray(res.results[c]["yT"])
        if c < 4:
            yp[c] = yT[:, 16:16 + 4096].T
        else:
            for s in range(2):
                ys[2 * (c - 4) + s] = yT[:, s * SEG + 16:(s + 1) * SEG].T
    return (yp, ys)
```
